# Optimizing a Trainium2 kernel written in Bass

```python
import math
import jax
import jax.numpy as jnp
from jax import lax
import numpy as np

D_MODEL = 1024
BATCH = 4
SEQ = 8192
DEPTH = 2

CTX_LEN = 256
GRID_W = 64
RMS_EPS = 1e-6
N_MOD = 6
D_FF = 4 * D_MODEL

RWKV_HEAD = 64
RWKV_W = D_MODEL // 4
RWKV_HEADS = RWKV_W // RWKV_HEAD
DECAY_LORA = 32
AAA_LORA = 32
GATE_LORA = 64
RWKV_COLS = 3 * RWKV_W + DECAY_LORA + AAA_LORA + GATE_LORA
LNX_EPS = 64e-5

GDN_HEAD = 128
GDN_W = D_MODEL // 2
GDN_HEADS = GDN_W // GDN_HEAD
GDN_CONV = 5
GDN_CHUNK = 64
GDN_COLS = 4 * GDN_W + 4 * GDN_HEADS

FNET_CH = 64
FNET_W = D_MODEL - RWKV_W - GDN_W
FNET_GROUPS = FNET_W // FNET_CH

MIX_W = RWKV_W + GDN_W + FNET_W
IN_COLS = RWKV_COLS + GDN_COLS + FNET_W

kernel_name = 'hybrid_rwkv7_gdn_fnet_dit_block'


def rmsnorm(x, g, eps=RMS_EPS):
    xf = x.astype(jnp.float32)
    y = xf * lax.rsqrt(jnp.mean(xf * xf, axis=-1, keepdims=True) + eps)
    return (y * g.astype(jnp.float32)).astype(x.dtype)


def l2norm(x, eps=1e-6):
    xf = x.astype(jnp.float32)
    return xf * lax.rsqrt(jnp.sum(xf * xf, axis=-1, keepdims=True) + eps)


def modulate(h, shift, scale):
    return h * (1 + scale) + shift


def qshift_grid(u, rows):
    b, t, ch = u.shape
    g = u.reshape(b, rows, GRID_W, ch // 4, 4)
    p = jnp.pad(g, ((0, 0), (1, 1), (1, 1), (0, 0), (0, 0)))
    sh = jnp.stack([p[:, 1:-1, :-2, :, 0], p[:, 1:-1, 2:, :, 1],
                    p[:, :-2, 1:-1, :, 2], p[:, 2:, 1:-1, :, 3]], axis=-1)
    return sh.reshape(b, t, ch)


def bishift_seq(u):
    b, t, ch = u.shape
    g = u.reshape(b, t, ch // 2, 2)
    p = jnp.pad(g, ((0, 0), (1, 1), (0, 0), (0, 0)))
    sh = jnp.stack([p[:, :-2, :, 0], p[:, 2:, :, 1]], axis=-1)
    return sh.reshape(b, t, ch)


def dwconv_centred(u, w):
    ch = u.shape[-1]
    kw = w.shape[0]
    return lax.conv_general_dilated(u, w[:, None, :].astype(u.dtype), window_strides=(1,),
                                    padding=[(kw // 2, kw // 2)],
                                    dimension_numbers=('NWC', 'WIO', 'NWC'),
                                    feature_group_count=ch)


def rwkv7_scan(r, w, k, v, kk, a, s0):
    def step(s, inp):
        r_t, w_t, k_t, v_t, kk_t, a_t = inp
        sa = jnp.einsum('bhvk,bhk->bhv', s, kk_t)
        s = (s * w_t[:, :, None, :] - sa[..., None] * (kk_t * a_t)[:, :, None, :]
             + v_t[..., None] * k_t[:, :, None, :])
        return s, jnp.einsum('bhvk,bhk->bhv', s, r_t)
    xs = tuple(jnp.moveaxis(z, 1, 0) for z in (r, w, k, v, kk, a))
    s, o = lax.scan(step, s0, xs)
    return jnp.moveaxis(o, 0, 1), s


def gdn_chunk_scan(q, k, v, g, beta, s0):
    b, t, h, dk = q.shape
    dv = v.shape[-1]
    cs = GDN_CHUNK
    n = t // cs

    def blocks(z):
        z = z.reshape((b, n, cs) + z.shape[2:])
        return jnp.moveaxis(jnp.moveaxis(z, 1, 0), 3, 2)

    qc, kc, vc, bc = blocks(q), blocks(k), blocks(v), blocks(beta)
    gc = jnp.cumsum(blocks(g), axis=-1)
    idx = jnp.arange(cs)
    incl = idx[:, None] >= idx[None, :]
    strict = idx[:, None] > idx[None, :]
    diff = gc[..., :, None] - gc[..., None, :]
    decay = jnp.where(incl, jnp.exp(jnp.where(incl, diff, 0.0)), 0.0)
    kb = kc * bc[..., None]
    lmat = jnp.where(strict, jnp.einsum('nbhik,nbhjk->nbhij', kb, kc) * decay, 0.0)
    amat = lmat + jnp.eye(cs, dtype=lmat.dtype)
    rhs = jnp.concatenate([vc * bc[..., None], kb * jnp.exp(gc)[..., None]], axis=-1)
    sol = lax.linalg.triangular_solve(amat, rhs, left_side=True, lower=True,
                                      unit_diagonal=True)
    uc, wc = sol[..., :dv], sol[..., dv:]
    qk = jnp.einsum('nbhik,nbhjk->nbhij', qc, kc) * decay

    def step(s, inp):
        q_i, k_i, u_i, w_i, g_i, qk_i = inp
        v_new = u_i - jnp.einsum('bhck,bhkv->bhcv', w_i, s)
        o = (jnp.einsum('bhck,bhkv->bhcv', q_i * jnp.exp(g_i)[..., None], s)
             + jnp.einsum('bhij,bhjv->bhiv', qk_i, v_new))
        g_last = g_i[..., -1]
        s = (s * jnp.exp(g_last)[..., None, None]
             + jnp.einsum('bhck,bhcv->bhkv', k_i * jnp.exp(g_last[..., None] - g_i)[..., None], v_new))
        return s, o

    s, o = lax.scan(step, s0, (qc, kc, uc, wc, gc, qk))
    o = jnp.moveaxis(jnp.moveaxis(o, 2, 3), 0, 1).reshape(b, t, h, dv)
    return o, s


def _flip(z, rev):
    return jnp.flip(z, axis=1) if rev else z


def rwkv_mixer(u, shift_fn, s_init, mu, w0, w_up, a0, a_up, g_up, k_k, k_a, r_k, lnx_g, lnx_b):
    b, t, _ = u.shape
    W, H, N = RWKV_W, RWKV_HEADS, RWKV_HEAD
    u = u.astype(jnp.float32)
    u = u + mu * (shift_fn(u) - u)
    r, k, v = u[..., :W], u[..., W:2 * W], u[..., 2 * W:3 * W]
    wd = u[..., 3 * W:3 * W + DECAY_LORA]
    ad = u[..., 3 * W + DECAY_LORA:3 * W + DECAY_LORA + AAA_LORA]
    gd = u[..., 3 * W + DECAY_LORA + AAA_LORA:]
    heads = lambda z: z.reshape(b, t, H, N)
    kk = l2norm(heads(k * k_k), eps=1e-12)
    gate = jax.nn.sigmoid(gd) @ g_up
    rh, vh = heads(r), heads(v)
    outs, bonus, states = [], [], []
    for d in range(2):
        rev = d == 1
        w_log = -jax.nn.softplus(-(w0[d] + jnp.tanh(wd) @ w_up[d])) - 0.5
        dec = jnp.exp(-jnp.exp(w_log))
        a = jax.nn.sigmoid(a0[d] + ad @ a_up[d])
        kd = heads(k * (1 + (a - 1) * k_a))
        o, s = rwkv7_scan(_flip(rh, rev), _flip(heads(dec), rev), _flip(kd, rev),
                          _flip(vh, rev), _flip(kk, rev), _flip(heads(a), rev), s_init[d])
        outs.append(_flip(o, rev))
        states.append(s)
        bonus.append(jnp.sum(rh * kd * r_k, axis=-1, keepdims=True) * vh)
    o = outs[0] + outs[1]
    mean = jnp.mean(o, axis=-1, keepdims=True)
    var = jnp.mean(jnp.square(o - mean), axis=-1, keepdims=True)
    o = ((o - mean) * lax.rsqrt(var + LNX_EPS)).reshape(b, t, W) * lnx_g + lnx_b
    o = (o + (bonus[0] + bonus[1]).reshape(b, t, W)) * gate
    return o, (states[0], states[1])


def gdn_mixer(u, s_init, conv_w, a_log, dt_bias, norm_g):
    b, t, _ = u.shape
    W, H, Dh = GDN_W, GDN_HEADS, GDN_HEAD
    qkv = jax.nn.silu(dwconv_centred(u[..., :3 * W], conv_w)).astype(jnp.float32)
    heads = lambda z: z.reshape(b, t, H, Dh)
    q = l2norm(heads(qkv[..., :W])) * (Dh ** -0.5)
    k = l2norm(heads(qkv[..., W:2 * W]))
    v = heads(qkv[..., 2 * W:])
    z = heads(u[..., 3 * W:4 * W]).astype(jnp.float32)
    sc = u[..., 4 * W:].astype(jnp.float32).reshape(b, t, 4, H)
    outs, states = [], []
    for d in range(2):
        rev = d == 1
        beta = jax.nn.sigmoid(sc[:, :, d])
        gl = -jnp.exp(a_log[d].astype(jnp.float32)) * jax.nn.softplus(sc[:, :, 2 + d] + dt_bias[d])
        o, s = gdn_chunk_scan(_flip(q, rev), _flip(k, rev), _flip(v, rev), _flip(gl, rev),
                              _flip(beta, rev), s_init[d])
        outs.append(_flip(o, rev))
        states.append(s)
    o = rmsnorm(outs[0] + outs[1], norm_g) * jax.nn.silu(z)
    return o.reshape(b, t, W), (states[0], states[1])


def fourier_mixer(u, w_f):
    b, t, _ = u.shape
    g = u.astype(jnp.float32).reshape(b, t, FNET_GROUPS, FNET_CH)
    f = jnp.fft.fft2(g, axes=(1, 3), norm='ortho').real
    return jnp.einsum('btgc,gcd->btgd', f, w_f).reshape(b, t, FNET_W)


def sq_relu_mlp(h, w1, w2):
    a = jax.nn.relu(h @ w1)
    return (a * a) @ w2


def setup_inputs(seed: int = 0) -> dict:
    key = jax.random.key(seed)
    ks = jax.random.split(key, 32)
    f32 = jnp.float32
    L, D = DEPTH, D_MODEL

    def nrm(i, shape, scale):
        return jax.random.normal(ks[i], shape, f32) * scale

    def uni(i, shape, lo, hi):
        return jax.random.uniform(ks[i], shape, f32, lo, hi)

    dt = jnp.exp(uni(23, (L, 2, GDN_HEADS), math.log(1e-3), math.log(1e-1)))
    return {
        'x': nrm(0, (BATCH, SEQ, D), 1.0),
        'c': nrm(1, (BATCH, D), 1.0),
        'ctx': nrm(2, (BATCH, CTX_LEN, D), 1.0),
        'c_ctx': nrm(3, (D,), 1.0),
        'norm1_g': 1.0 + nrm(4, (L, D), 0.02),
        'norm2_g': 1.0 + nrm(5, (L, D), 0.02),
        'w_mod': nrm(6, (L, D, N_MOD * D), 0.5 * D ** -0.5),
        'b_mod': nrm(7, (L, N_MOD * D), 0.02),
        'w_in': nrm(8, (L, D, IN_COLS), D ** -0.5),
        'w_out': nrm(9, (L, MIX_W, D), MIX_W ** -0.5),
        'rk_mu': uni(10, (L, RWKV_COLS), 0.0, 1.0),
        'rk_w0': uni(11, (L, 2, RWKV_W), -6.0, -0.5),
        'rk_w_up': nrm(12, (L, 2, DECAY_LORA, RWKV_W), 0.5 * DECAY_LORA ** -0.5),
        'rk_a0': nrm(13, (L, 2, RWKV_W), 0.1),
        'rk_a_up': nrm(14, (L, 2, AAA_LORA, RWKV_W), 0.5 * AAA_LORA ** -0.5),
        'rk_g_up': nrm(15, (L, GATE_LORA, RWKV_W), GATE_LORA ** -0.5),
        'rk_k_k': 0.85 + nrm(16, (L, RWKV_W), 0.02),
        'rk_k_a': 1.0 + nrm(17, (L, RWKV_W), 0.02),
        'rk_r_k': nrm(18, (L, RWKV_HEADS, RWKV_HEAD), 0.1),
        'rk_lnx_g': 1.0 + nrm(19, (L, RWKV_W), 0.02),
        'rk_lnx_b': nrm(20, (L, RWKV_W), 0.02),
        'gd_conv_w': nrm(21, (L, GDN_CONV, 3 * GDN_W), GDN_CONV ** -0.5),
        'gd_a_log': jnp.log(uni(22, (L, 2, GDN_HEADS), 1.0, 16.0)),
        'gd_dt_bias': dt + jnp.log(-jnp.expm1(-dt)),
        'gd_norm_g': 1.0 + nrm(24, (L, GDN_HEAD), 0.02),
        'fn_w': nrm(25, (L, FNET_GROUPS, FNET_CH, FNET_CH), FNET_CH ** -0.5),
        'mlp_w1': nrm(26, (L, D, D_FF), D ** -0.5),
        'mlp_w2': nrm(27, (L, D_FF, D), D_FF ** -0.5),
        'final_g': 1.0 + nrm(28, (D,), 0.02),
    }


def reference(x, c, ctx, c_ctx, norm1_g, norm2_g, w_mod, b_mod, w_in, w_out,
              rk_mu, rk_w0, rk_w_up, rk_a0, rk_a_up, rk_g_up, rk_k_k, rk_k_a, rk_r_k,
              rk_lnx_g, rk_lnx_b, gd_conv_w, gd_a_log, gd_dt_bias, gd_norm_g, fn_w,
              mlp_w1, mlp_w2, final_g):
    b = x.shape[0]
    rows = x.shape[1] // GRID_W
    grid_shift = lambda u: qshift_grid(u, rows)
    r_end = RWKV_COLS
    g_end = RWKV_COLS + GDN_COLS
    zero_r = jnp.zeros((b, RWKV_HEADS, RWKV_HEAD, RWKV_HEAD), jnp.float32)
    zero_g = jnp.zeros((b, GDN_HEADS, GDN_HEAD, GDN_HEAD), jnp.float32)
    xl, xc = x, ctx
    for l in range(DEPTH):
        last = l == DEPTH - 1
        m = (jax.nn.silu(c) @ w_mod[l] + b_mod[l]).reshape(b, N_MOD, 1, D_MODEL)
        mc = (jax.nn.silu(c_ctx) @ w_mod[l] + b_mod[l]).reshape(N_MOD, D_MODEL)
        rk = (rk_mu[l], rk_w0[l], rk_w_up[l], rk_a0[l], rk_a_up[l], rk_g_up[l],
              rk_k_k[l], rk_k_a[l], rk_r_k[l], rk_lnx_g[l], rk_lnx_b[l])
        gd = (gd_conv_w[l], gd_a_log[l], gd_dt_bias[l], gd_norm_g[l])
        uc = modulate(rmsnorm(xc, norm1_g[l]), mc[0], mc[1]) @ w_in[l]
        yrc, sr = rwkv_mixer(uc[..., :r_end], bishift_seq, (zero_r, zero_r), *rk)
        ygc, sg = gdn_mixer(uc[..., r_end:g_end], (zero_g, zero_g), *gd)
        ux = modulate(rmsnorm(xl, norm1_g[l]), m[:, 0], m[:, 1]) @ w_in[l]
        yrx, _ = rwkv_mixer(ux[..., :r_end], grid_shift, sr, *rk)
        ygx, _ = gdn_mixer(ux[..., r_end:g_end], sg, *gd)
        yfx = fourier_mixer(ux[..., g_end:], fn_w[l])
        yx = jnp.concatenate([yrx, ygx, yfx], axis=-1).astype(xl.dtype) @ w_out[l]
        xl = xl + m[:, 2] * yx
        hx = modulate(rmsnorm(xl, norm2_g[l]), m[:, 3], m[:, 4])
        xl = xl + m[:, 5] * sq_relu_mlp(hx, mlp_w1[l], mlp_w2[l])
        if not last:
            yfc = fourier_mixer(uc[..., g_end:], fn_w[l])
            yc = jnp.concatenate([yrc, ygc, yfc], axis=-1).astype(xc.dtype) @ w_out[l]
            xc = xc + mc[2] * yc
            hc = modulate(rmsnorm(xc, norm2_g[l]), mc[3], mc[4])
            xc = xc + mc[5] * sq_relu_mlp(hc, mlp_w1[l], mlp_w2[l])
    return rmsnorm(xl, final_g)
```

```python
import math
from contextlib import ExitStack
import numpy as np
import concourse.bass as bass
import concourse.mybir as mybir
from concourse.bass_utils import run_bass_kernel_spmd

F32 = mybir.dt.float32
BF16 = mybir.dt.bfloat16
ALU = mybir.AluOpType
AF = mybir.ActivationFunctionType
AX = mybir.AxisListType

D = 1024
NTOK = 1424
NFEAT = 1792
RING = 12


class V:
    __slots__ = ("tile", "ap")

    def __init__(self, tile, ap):
        self.tile = tile
        self.ap = ap

    def __getitem__(self, key):
        return V(self.tile, self.ap[key])

    def rr(self, pat, **kw):
        return V(self.tile, self.ap.rearrange(pat, **kw))

    def bc(self, shape):
        return V(self.tile, self.ap.to_broadcast(list(shape)))

    def un(self, axis):
        return V(self.tile, self.ap.unsqueeze(axis))

    def bitcast(self, dt):
        return V(self.tile, self.ap.bitcast(dt))


class Tile:
    __slots__ = ("t", "lw", "rd", "name", "excl")

    def __init__(self, t, name, excl=False):
        self.t = t
        self.lw = None
        self.rd = {}
        self.name = name
        self.excl = excl

    def __getitem__(self, key):
        return V(self, self.t[key])


class Eng:
    def __init__(self, name, be, sem):
        self.name, self.be, self.sem = name, be, sem
        self.cnt = 0
        self.seen = {}


class Ring:
    def __init__(self, sems):
        self.sems = sems
        self.vals = [0] * len(sems)
        self.i = 0


class K:
    def __init__(self, nc, es):
        self.nc = nc
        self.es = es
        self.E = {}
        for name, be in (("pe", nc.tensor), ("act", nc.scalar), ("dve", nc.vector),
                         ("pool", nc.gpsimd), ("sp", nc.sync)):
            sem = es.enter_context(nc.semaphore("s_" + name))
            self.E[name] = Eng(name, be, sem)
        self.dq = {}
        for q in ("sp", "pool"):
            self.dq[q] = Ring([es.enter_context(nc.semaphore("d_%s%d" % (q, i))) for i in range(RING)])
        self.nid = 0
        self.psb = []
        self.psi = 0
        for e in self.E.values():
            nc.sync.sem_clear(e.sem)
        for ring in self.dq.values():
            for sm in ring.sems:
                nc.sync.sem_clear(sm)
        nc.all_engine_barrier()

    def sb(self, es, shape, dt=F32, name=None):
        self.nid += 1
        name = "%s_%d" % (name or "t", self.nid)
        return Tile(es.enter_context(self.nc.sbuf_tensor(name, list(shape), dt)), name)

    def init_psum(self, es):
        for i in range(8):
            self.psb.append(Tile(es.enter_context(self.nc.psum_tensor("ps%d" % i, [128, 512], F32)), "ps%d" % i, True))

    def ps(self):
        t = self.psb[self.psi % 8]
        self.psi += 1
        return t

    def _wait(self, e, tok, raw=False):
        sem, val, key = tok
        if key == e.name and e.name == "pe":
            return
        if e.seen.get(key, 0) >= val:
            return
        e.be.wait_ge(sem, val)
        e.seen[key] = val

    def _deps(self, e, reads, writes):
        for t in reads:
            if t.lw is not None:
                self._wait(e, t.lw, raw=True)
        for t in writes:
            if t.lw is not None:
                self._wait(e, t.lw)
            for tok in t.rd.values():
                self._wait(e, tok)

    def op(self, en, meth, *args, **kw):
        e = self.E[en]
        reads, writes = [], []

        def cv(a, is_out):
            if isinstance(a, V):
                (writes if is_out else reads).append(a.tile)
                return a.ap
            return a
        args2 = [cv(a, i == 0) for i, a in enumerate(args)]
        kw2 = {k_: cv(v, k_ in ("out", "accum_out")) for k_, v in kw.items()}
        writes = writes + [t for t in reads if t.excl]
        reads = [t for t in reads if not t.excl]
        self._deps(e, reads, writes)
        ins = getattr(e.be, meth)(*args2, **kw2)
        e.cnt += 1
        ins.then_inc(e.sem, 1)
        tok = (e.sem, e.cnt, e.name)
        for t in writes:
            t.lw = tok
            t.rd = {}
        for t in reads:
            t.rd[e.name] = tok
        return ins

    def dma(self, out, in_, q="sp"):
        e = self.E[q]
        ring = self.dq[q]
        slot = ring.i % RING
        ring.i += 1
        sem, prev = ring.sems[slot], ring.vals[slot]
        key = "%s_d%d" % (q, slot)
        reads, writes = [], []
        o = out
        i = in_
        if isinstance(out, V):
            writes.append(out.tile)
            o = out.ap
        if isinstance(in_, V):
            reads.append(in_.tile)
            i = in_.ap
        self._deps(e, reads, writes)
        if prev:
            self._wait(e, (sem, prev, key))
        ins = e.be.dma_start(out=o, in_=i)
        ins.then_inc(sem, 16)
        ring.vals[slot] = prev + 16
        tok = (sem, prev + 16, key)
        for t in writes:
            t.lw = tok
            t.rd = {}
        for t in reads:
            t.rd[key] = tok

    def barrier(self):
        toks = [(e.sem, e.cnt, e.name) for e in self.E.values() if e.cnt]
        for q, ring in self.dq.items():
            for s in range(RING):
                if ring.vals[s]:
                    toks.append((ring.sems[s], ring.vals[s], "%s_d%d" % (q, s)))
        for e in self.E.values():
            for tok in toks:
                self._wait(e, tok)

    def mm(self, out, lhsT, rhs, start=True, stop=True):
        return self.op("pe", "matmul", out, lhsT, rhs, start=start, stop=stop)

    def tr(self, out, in_, ident):
        return self.op("pe", "transpose", out, in_, ident)

    def act(self, out, in_, func, bias=0.0, scale=1.0, accum_out=None):
        kw = {}
        if accum_out is not None:
            kw["accum_out"] = accum_out
        return self.op("act", "activation", out, in_, func, bias=bias, scale=scale, **kw)

    def tt(self, out, in0, in1, op, eng="dve"):
        return self.op(eng, "tensor_tensor", out, in0, in1, op)

    def ts(self, out, in0, s1, s2, op0, op1=None, eng="dve"):
        if op1 is None:
            return self.op(eng, "tensor_scalar", out, in0, s1, None, op0)
        return self.op(eng, "tensor_scalar", out, in0, s1, s2, op0, op1)

    def stt(self, out, in0, scalar, in1, op0, op1):
        return self.op("dve", "scalar_tensor_tensor", out, in0, scalar, in1, op0, op1)

    def cp(self, out, in_, eng="dve"):
        if eng == "act":
            return self.act(out, in_, AF.Copy)
        return self.op(eng, "tensor_copy", out, in_)

    def memset(self, out, val, eng="dve"):
        return self.op(eng, "memset", out, val)

    def rsqrt(self, out, in_, scale, eps, tmp):
        self.act(tmp, in_, AF.Sqrt, bias=eps, scale=scale)
        self.op("dve", "reciprocal", out, tmp)


def host_consts(T, TC):
    p = np.arange(128)
    c = {}
    c["ident"] = np.eye(128, dtype=np.float32)
    c["ones"] = np.ones((128, 128), np.float32)
    le = (p[:, None] <= p[None, :]).astype(np.float32)
    lt = (p[:, None] < p[None, :]).astype(np.float32)
    ge = le.T.copy()
    gt = lt.T.copy()
    c["tri"] = np.stack([le, ge])
    c["msk2"] = np.stack([np.concatenate([lt, le], 1), np.concatenate([gt, ge], 1)])
    c["nmsk"] = np.stack([gt, lt])
    c["imsk_ts"] = np.stack([ge, le])
    m = np.ones((8, 128, 4), np.float32)
    lat = np.ones((128, 4), np.float32)
    lat[:, 0] = (p % 64 != 0)
    lat[:, 1] = (p % 64 != 63)
    m[0] = lat
    m[1] = lat; m[1, :, 2] = (p >= 64)
    m[2] = lat; m[2, :, 3] = (p < 64)
    m[3] = lat; m[3, :, 2] = (p >= 64); m[3, :, 3] = (p < 64)
    m[5, :, 0] = (p != 0); m[5, :, 2] = (p != 0)
    m[6, :, 1] = (p != 127); m[6, :, 3] = (p != 127)
    m[7] = m[5] * m[6]
    c["smask"] = m
    k64 = np.arange(64)
    ang = 2 * np.pi * np.outer(k64, k64) / 64.0
    C64, S64 = np.cos(ang), np.sin(ang)
    z = np.zeros((64, 64))
    c["c64bd"] = np.block([[C64, z], [z, C64]]).astype(np.float32)
    c["s64bd"] = np.block([[S64, z], [z, S64]]).astype(np.float32)
    T2 = T // 128
    a1 = 2 * np.pi * np.outer(p, p) / 128.0
    c["c1"] = np.cos(a1).astype(np.float32)
    c["s1"] = np.sin(a1).astype(np.float32)
    tw = 2 * np.pi * np.outer(p, np.arange(T2)) / float(T)
    c["twc"] = np.cos(tw).astype(np.float32)
    c["tws"] = np.sin(tw).astype(np.float32)
    a3 = 2 * np.pi * np.outer(np.arange(T2), np.arange(T2)) / float(T2)
    nrm = 1.0 / math.sqrt(T * 64.0)
    c["c3"] = (np.cos(a3) * nrm).astype(np.float32)
    c["s3"] = (np.sin(a3) * nrm).astype(np.float32)
    tc = np.arange(TC)
    ac = 2 * np.pi * np.outer(tc, tc) / float(TC)
    nrc = 1.0 / math.sqrt(TC * 64.0)
    c["cc"] = (np.cos(ac) * nrc).astype(np.float32).reshape(TC // 128, 128, TC)
    c["sc_"] = (np.sin(ac) * nrc).astype(np.float32).reshape(TC // 128, 128, TC)
    return c


PARAMS = [
    ("norm_g", (16, 128)),
    ("w_mod", (1024, 6144)),
    ("b_mod", (1, 6144)),
    ("w_in", (1024, 3216)),
    ("w_out", (1024, 1024)),
    ("rk_mu", (1, 896)),
    ("rk_wup", (2, 33, 256)),
    ("rk_aup", (2, 33, 256)),
    ("rk_gup", (64, 256)),
    ("rk_vec", (5, 256)),
    ("gd_convw", (1536, 5)),
    ("gd_sc", (1, 16)),
    ("gd_ng", (1, 128)),
    ("fn_wbd", (2, 128, 128)),
    ("mlp_w1", (1024, 4096)),
    ("mlp_w2", (4096, 1024)),
]


class Cut(Exception):
    pass


def build(T, TC, L, debug=(), upto=99, cut=0):
    nc = bass.Bass("TRN2", target_bir_lowering=False)
    NT, NTC = T // 128, TC // 128
    T2 = T // 128
    cst = host_consts(T, TC)

    def din(name, shape, dt=F32):
        return nc.dram_tensor(name, list(shape), dt, kind="ExternalInput").ap()

    def dscr(name, shape, dt=F32):
        kind = "ExternalOutput" if name in debug else "Internal"
        return nc.dram_tensor(name, list(shape), dt, kind=kind).ap()

    I = {}
    I["x"] = din("x", (T, D))
    I["ctx"] = din("ctx", (TC, D))
    I["cvec"] = din("cvec", (16, 128))
    I["final_g"] = din("final_g", (1, D))
    for name, shp in PARAMS:
        I[name] = din(name, (L,) + shp)
    C = {k_: din("c_" + k_, v.shape) for k_, v in cst.items()}
    OUT = nc.dram_tensor("out", [T, D], F32, kind="ExternalOutput").ap()

    class Seq:
        pass
    seqs = []
    for nm, n, xin in (("c", TC, I["ctx"]), ("l", T, I["x"])):
        s = Seq()
        s.nm, s.n, s.nt, s.xin = nm, n, n // 128, xin
        s.isctx = nm == "c"
        s.X = dscr("X_" + nm, (n, D))
        s.UT = dscr("UT_" + nm, (n, NTOK))
        s.UQ = dscr("UQ_" + nm, (1536, n + 4))
        s.UF = dscr("UF_" + nm, (256, n))
        s.OR = [dscr("OR%d_%s" % (d, nm), (n, 256)) for d in range(2)]
        s.BON = [dscr("BON%d_%s" % (d, nm), (n, 256)) for d in range(2)]
        s.GATE = dscr("GATE_" + nm, (n, 256))
        s.OG = [dscr("OG%d_%s" % (d, nm), (n, 512)) for d in range(2)]
        s.AB = dscr("AB_" + nm, (n, 512))
        s.YF = dscr("YF_" + nm, (n, 256))
        seqs.append(s)
    SC, SL = seqs
    ZS = dscr("ZS", (128, T2, 2, 256))

    es = ExitStack()
    k = K(nc, es)
    k.init_psum(es)

    def dbgout(name, view, shape):
        if name in debug:
            t = nc.dram_tensor(name, list(shape), F32, kind="ExternalOutput").ap()
            k.dma(t, view, "sp")

    def cload(name, shape, src=None, q="sp"):
        t = k.sb(es, shape, F32, name)
        k.dma(t[:], C[name] if src is None else src, q)
        return t
    ident = cload("ident", (128, 128))
    ones = cload("ones", (128, 128))
    tri = [cload("tri", (128, 128), C["tri"][d]) for d in range(2)]
    msk2 = [cload("msk2", (128, 256), C["msk2"][d]) for d in range(2)]
    nmsk = [cload("nmsk", (128, 128), C["nmsk"][d]) for d in range(2)]
    imsk_ts = [cload("imsk_ts", (128, 128), C["imsk_ts"][d]) for d in range(2)]
    smask = [cload("smask", (128, 4), C["smask"][i]) for i in range(8)]
    identb = k.sb(es, (128, 128), BF16, "identb")
    k.dma(identb[:], C["ident"], "pool")
    zeros = k.sb(es, (128, 512), F32, "zeros")
    k.memset(zeros[:], 0.0)
    warm = k.sb(es, (128, 8), F32, "warm")
    k.act(warm[:], zeros[:, 0:8], AF.Silu)
    for s in seqs:
        for c0 in (0, s.n + 2):
            k.dma(s.UQ.rearrange("(c p) t -> p c t", p=128)[:, :, c0:c0 + 2],
                  zeros[:, 0:24].rr("p (c t) -> p c t", t=2), "sp")
    G1 = [k.sb(es, (128, 8), F32, "G1") for _ in range(2)]
    SH1 = [k.sb(es, (128, 8), F32, "SH1") for _ in range(2)]
    G2 = [k.sb(es, (128, 8), F32, "G2") for _ in range(2)]
    SH2 = [k.sb(es, (128, 8), F32, "SH2") for _ in range(2)]
    M2 = [k.sb(es, (128, D), F32, "M2") for _ in range(2)]
    M5 = [k.sb(es, (128, D), F32, "M5") for _ in range(2)]
    scT = k.sb(es, (128, 16), F32, "scT")
    with ExitStack() as e0:
        cv = k.sb(e0, (16, 128), F32, "cv")
        k.dma(cv[:], I["cvec"])
        p_ = k.ps()
        k.tr(p_[:, 0:16], cv[:], ident[0:16, 0:16])
        k.act(scT[:], p_[:, 0:16], AF.Silu)
    dbgout("D_scT", scT[:], (128, 16))

    HR = [[k.sb(es, (64, 64), F32, "HR") for h in range(4)] for d in range(2)]
    HG = [[k.sb(es, (128, 128), F32, "HG") for h in range(4)] for d in range(2)]

    def pass0_mod(l):
        with ExitStack() as e:
            mrow = [k.sb(e, (1, 6144), F32, "mrow") for _ in range(2)]
            bm = k.sb(e, (1, 6144), F32, "bm")
            raw = k.sb(e, (1, 6144), F32, "raw")
            k.dma(bm[:], I["b_mod"][l])
            wmb = [k.sb(e, (128, 8, 512), F32, "wmb") for _ in range(2)]
            for blk in range(12):
                w = wmb[blk % 2]
                k.dma(w[:], I["w_mod"][l].rearrange("(k p) n -> p k n", p=128)[:, :, blk * 512:(blk + 1) * 512],
                      "sp" if blk % 2 == 0 else "pool")
                if l == 0 and blk < 2:
                    dbgout("D_w%d" % blk, w[:], (128, 8, 512))
                for wh in range(2):
                    p_ = k.ps()
                    for kc in range(8):
                        col = (0 if wh == 0 else 8) + kc
                        k.mm(p_[0:1, :], scT[:, col:col + 1], w[:, kc, :], start=(kc == 0), stop=(kc == 7))
                    if wh == 0 and l == 0 and "D_raw" in debug:
                        k.cp(raw[:, blk * 512:(blk + 1) * 512], p_[0:1, :])
                    k.tt(mrow[wh][:, blk * 512:(blk + 1) * 512], p_[0:1, :], bm[:, blk * 512:(blk + 1) * 512], ALU.add)
            if l == 0:
                dbgout("D_mrow", mrow[0][:], (1, 6144))
                dbgout("D_raw", raw[:], (1, 6144))
                dbgout("D_bm", bm[:], (1, 6144))
            ng = k.sb(e, (16, 128), F32, "ng")
            k.dma(ng[:], I["norm_g"][l])
            gcol = k.sb(e, (128, 16), F32, "gcol")
            p_ = k.ps()
            k.tr(p_[:, 0:16], ng[:], ident[0:16, 0:16])
            k.cp(gcol[:], p_[:, 0:16])
            for wh in range(2):
                mc = k.sb(e, (128, 48), F32, "mc")
                p_ = k.ps()
                for j in range(48):
                    k.mm(p_[:, j:j + 1], mrow[wh][:, j * 128:(j + 1) * 128], ones[0:1, 0:1])
                k.cp(mc[:], p_[:, 0:48])
                k.stt(G1[wh][:], mc[:, 8:16], 1.0, gcol[:, 0:8], ALU.add, ALU.mult)
                k.stt(G2[wh][:], mc[:, 32:40], 1.0, gcol[:, 8:16], ALU.add, ALU.mult)
                k.cp(SH1[wh][:], mc[:, 0:8])
                k.cp(SH2[wh][:], mc[:, 24:32])
                if l == 0 and wh == 0:
                    dbgout("D_mc", mc[:], (128, 48))
                    dbgout("D_G1", G1[0][:], (128, 8))
                for dst, off in ((M2[wh], 2048), (M5[wh], 5120)):
                    for hb in range(2):
                        p_ = k.ps()
                        k.mm(p_[:, :], ones[0:1, :], mrow[wh][:, off + hb * 512: off + (hb + 1) * 512])
                        k.cp(dst[:, hb * 512:(hb + 1) * 512], p_[:, :], eng="act")

    dbgdone = []

    def norm_to_hT(e, xt, hT, col0, Gc, SHc, tmp_pool):
        junk, ssq, sq, rstd, xnb = tmp_pool
        k.act(junk[:], xt[:], AF.Square, accum_out=ssq[:])
        k.rsqrt(rstd[:], ssq[:], 1.0 / D, 1e-6, sq[:])
        k.ts(xnb[:], xt[:], rstd[:, 0:1], None, ALU.mult)
        if "D_ssq" in debug and not dbgdone:
            dbgdone.append(1)
            dbgout("D_ssq", ssq[:], (128, 1))
            dbgout("D_rstd", rstd[:], (128, 1))
            dbgout("D_junk", junk[:], (128, D))
        p_ = k.ps()
        pb = p_[:, :].bitcast(BF16)
        for kc in range(8):
            k.tr(pb[:, kc * 128:(kc + 1) * 128], xnb[:, kc * 128:(kc + 1) * 128], identb[:])
        for kc in range(8):
            k.act(hT[:, kc, col0:col0 + 128], pb[:, kc * 128:(kc + 1) * 128], AF.Identity,
                  bias=SHc[:, kc:kc + 1], scale=Gc[:, kc:kc + 1])

    def pass1(l, sqs):
        with ExitStack() as e:
            win = k.sb(e, (128, 8, 3216), BF16, "win")
            wsrc = I["w_in"][l].rearrange("(k p) n -> p k n", p=128)
            for kc in range(8):
                for h in range(2):
                    k.dma(win[:, kc, h * 1608:(h + 1) * 1608], wsrc[:, kc, h * 1608:(h + 1) * 1608], "pool")
            hT = [k.sb(e, (128, 8, 512), BF16, "hT") for _ in range(2)]
            xts = [k.sb(e, (128, D), F32, "xt") for _ in range(2)]
            tmp_pool = (k.sb(e, (128, D), F32, "junk"), k.sb(e, (128, 1), F32, "ssq"), k.sb(e, (128, 1), F32, "sq"),
                        k.sb(e, (128, 1), F32, "rstd"), k.sb(e, (128, D), BF16, "xnb"))
            ust = [k.sb(e, (128, NTOK), F32, "ust") for _ in range(2)]
            fst = [k.sb(e, (128, 512), F32, "fst") for _ in range(3)]
            gi = 0
            for s in sqs:
                wh = 1 if s.isctx else 0
                xsrc = s.xin if l == 0 else s.X
                GS = min(512, s.n)
                ntg = GS // 128
                for g0 in range(0, s.n, GS):
                    h_ = hT[gi % 2]
                    gi += 1
                    for j in range(ntg):
                        xt = xts[j % 2]
                        k.dma(xt[:], xsrc[g0 + j * 128: g0 + (j + 1) * 128, :], "sp")
                        norm_to_hT(e, xt, h_, j * 128, G1[wh], SH1[wh], tmp_pool)
                    for j in range(ntg):
                        u = ust[j % 2]
                        for (c0, c1) in ((0, 512), (512, 1024), (1024, NTOK)):
                            p_ = k.ps()
                            for kc in range(8):
                                k.mm(p_[:, 0:c1 - c0], h_[:, kc, j * 128:(j + 1) * 128], win[:, kc, c0:c1],
                                     start=(kc == 0), stop=(kc == 7))
                            k.cp(u[:, c0:c1], p_[:, 0:c1 - c0], eng="act" if c0 == 512 else "dve")
                        k.dma(s.UT[g0 + j * 128: g0 + (j + 1) * 128, :], u[:], "pool")
                    for ct in range(14):
                        p_ = k.ps()
                        for kc in range(8):
                            k.mm(p_[:, 0:GS], win[:, kc, NTOK + ct * 128: NTOK + (ct + 1) * 128], h_[:, kc, 0:GS],
                                 start=(kc == 0), stop=(kc == 7))
                        f = fst[ct % 3]
                        k.cp(f[:, 0:GS], p_[:, 0:GS], eng="act" if ct % 2 else "dve")
                        if ct < 12:
                            k.dma(s.UQ[ct * 128:(ct + 1) * 128, 2 + g0: 2 + g0 + GS], f[:, 0:GS], "sp")
                        else:
                            k.dma(s.UF[(ct - 12) * 128:(ct - 11) * 128, g0: g0 + GS], f[:, 0:GS], "sp")

    def inv_chain(e, N, M, pool):
        R = pool["R"]
        k.tt(R[:], M[:], ident[:], ALU.add)
        Nk, Mk = N, M
        for lev in range(7):
            if lev == 0:
                pn = k.ps()
                k.mm(pn[:, 0:128], Mk[:], Nk[:])
                pm = k.ps()
                k.mm(pm[:, 0:128], Nk[:], Mk[:])
                Nn, Mn = pool["N"][0], pool["M"][0]
                k.cp(Nn[:], pn[:, 0:128], eng="act")
                k.cp(Mn[:], pm[:, 0:128], eng="dve")
                Nk, Mk = Nn, Mn
                continue
            last = lev == 6
            pr = k.ps()
            if not last:
                rm = pool["RM"]
                k.cp(rm[:, 0:128], R[:], eng="act")
                k.cp(rm[:, 128:256], Mk[:], eng="act")
                k.mm(pr[:, 0:256], Nk[:], rm[:])
                pn = k.ps()
                k.mm(pn[:, 0:128], Mk[:], Nk[:])
                Nn, Mn = pool["N"][lev % 2], pool["M"][lev % 2]
                k.tt(R[:], R[:], pr[:, 0:128], ALU.add)
                k.cp(Mn[:], pr[:, 128:256], eng="act")
                k.cp(Nn[:], pn[:, 0:128], eng="dve")
                Nk, Mk = Nn, Mn
            else:
                k.mm(pr[:, 0:128], Nk[:], R[:])
                Tt = pool["Tt"]
                k.tt(Tt[:], R[:], pr[:, 0:128], ALU.add)
        return pool["Tt"]

    def seq_step(H, Kd, Vd, Tt, Yrhs, WT, QT, AT, KB, dec, otile, ocol, Ut, ext_o=None, ext_h=None):
        pu = k.ps()
        k.mm(pu[:, 0:Vd], Tt[:], Yrhs, start=True, stop=False)
        k.mm(pu[:, 0:Vd], WT, H[:], start=False, stop=True)
        k.cp(Ut[:, 0:Vd], pu[:, 0:Vd], eng="act")
        po = k.ps()
        k.mm(po[:, 0:Vd], QT, H[:], start=True, stop=False)
        k.mm(po[:, 0:Vd], AT, Ut[:, 0:Vd], start=False, stop=(ext_o is None))
        if ext_o is not None:
            k.mm(po[:, 0:Vd], ext_o[0], ext_o[1], start=False, stop=True)
        k.cp(otile[:, ocol:ocol + Vd], po[:, 0:Vd], eng="act")
        ph = k.ps()
        k.mm(ph[0:Kd, 0:Vd], KB, Ut[:, 0:Vd], start=True, stop=(ext_h is None))
        if ext_h is not None:
            k.mm(ph[0:Kd, 0:Vd], ext_h[0], ext_h[1], start=False, stop=True)
        k.stt(H[:], H[:], dec, ph[0:Kd, 0:Vd], ALU.mult, ALU.add)

    def ck(n):
        if cut == n:
            raise Cut()

    def scan_pass(l, d):
        with ExitStack() as e:
            def bcast(name, src, n):
                t = k.sb(e, (128, n), F32, name)
                k.dma(t[:], src.partition_broadcast(128), "sp")
                return t
            mu = bcast("mu", I["rk_mu"][l][0], 896)
            rv = [bcast("rv", I["rk_vec"][l][j], 256) for j in range(3)]
            KKb, KAb, RKb = rv
            gsc = bcast("gsc", I["gd_sc"][l][0], 16)
            nega = k.sb(e, (128, 4), F32, "nega")
            k.act(nega[:], gsc[:, 4 * d:4 * d + 4], AF.Exp)
            k.ts(nega[:], nega[:], -1.0, None, ALU.mult)
            dtb = gsc[:, 8 + 4 * d: 12 + 4 * d]
            wup = k.sb(e, (33, 256), F32, "wup")
            k.dma(wup[:], I["rk_wup"][l][d])
            aup = k.sb(e, (33, 256), F32, "aup")
            k.dma(aup[:], I["rk_aup"][l][d])
            gup = k.sb(e, (64, 256), F32, "gup")
            k.dma(gup[:], I["rk_gup"][l])
            cw = k.sb(e, (128, 12, 5), F32, "cw")
            k.dma(cw[:], I["gd_convw"][l].rearrange("(c p) j -> p c j", p=128))
            def W(shape, name, dt=F32):
                return k.sb(e, shape, dt, name)
            u = W((128, 896), "u"); sh = W((128, 896), "sh"); xs = W((128, 896), "xs"); t896 = W((128, 896), "t896")
            k.memset(sh[:], 0.0)
            ltw = W((33, 128), "ltw"); lta = W((33, 128), "lta"); ltg = W((64, 128), "ltg")
            k.memset(ltw[:], 1.0)
            k.memset(lta[:], 1.0)
            lw = W((128, 256), "lw"); alpha = W((128, 256), "alpha"); gate = W((128, 256), "gate")
            kkr = W((128, 256), "kkr"); t256 = W((128, 256), "t256"); t256b = W((128, 256), "t256b")
            s4 = W((128, 4), "s4"); s4b = W((128, 4), "s4b"); s4c = W((128, 4), "s4c")
            kk = W((128, 256), "kk"); kd = W((128, 256), "kd"); bb = W((128, 256), "bb"); bon = W((128, 256), "bon")
            clw = W((128, 256), "clw"); Pin = W((128, 256), "Pin"); Pex = W((128, 256), "Pex")
            Pinv = W((128, 256), "Pinv"); Prem = W((128, 256), "Prem")
            Ah = W((128, 256), "Ah"); Rh = W((128, 256), "Rh"); Bc = W((128, 256), "Bc"); Kc = W((128, 256), "Kc")
            Bt = W((128, 256), "Bt"); Kt = W((128, 256), "Kt")
            pcol = W((64, 4), "pcol")
            FT = [W((64, 512), "FT") for _ in range(2)]
            Nm = [W((128, 128), "Nm") for _ in range(2)]; Mm = [W((128, 128), "Mm") for _ in range(2)]
            AK = [W((128, 256), "AK") for _ in range(2)]; AB_ = [W((128, 256), "ABm") for _ in range(2)]
            pool = {"R": W((128, 128), "R"), "N": [W((128, 128), "Nn") for _ in range(2)],
                    "M": [W((128, 128), "Mn") for _ in range(2)], "RM": W((128, 256), "RM"), "Tt": W((128, 128), "Tt")}
            WTt = [W((128, 128), "WT") for _ in range(2)]
            Ysb = [W((128, 128), "Ysb") for _ in range(2)]
            Ut = [W((128, 128), "Ut") for _ in range(2)]
            orw = [W((128, 256), "orw") for _ in range(2)]
            ogd = [W((128, 512), "ogd") for _ in range(2)]
            uq = W((128, 12, 132), "uq"); acc = W((128, 12, 128), "acc"); tq = W((128, 12, 128), "tq")
            qkv = W((128, 12, 128), "qkv"); sq8 = W((128, 8, 128), "sq8"); rs8 = W((128, 8, 128), "rs8")
            ktok = W((128, 4, 128), "ktok"); vtok = W((128, 4, 128), "vtok")
            sct = W((128, 16), "sct"); beta = W((128, 4), "beta"); gg = W((128, 4), "gg"); gcc = W((128, 4), "gcc")
            ngc = W((128, 4), "ngc"); egc = W((128, 4), "egc"); bege = W((128, 4), "bege"); erem = W((128, 4), "erem")
            etot = W((128, 4), "etot"); nbeta = W((128, 4), "nbeta")
            trg = W((128, 128), "trg"); Dm = W((128, 128), "Dm"); DTm = W((128, 128), "DTm"); EG = W((128, 128), "EG")
            kbe = W((128, 128), "kbe"); vb = W((128, 128), "vb"); KBt = W((128, 128), "KBt"); QTt = W((128, 128), "QTt")
            ATt = W((128, 128), "ATt")

            for h in range(4):
                k.memset(HR[d][h][:], 0.0)
                k.memset(HG[d][h][:], 0.0)

            order = []
            for s in (SC, SL):
                tl = list(range(s.nt))
                if d == 1:
                    tl = tl[::-1]
                order += [(s, i) for i in tl]
            it = 0
            ck(1)
            for (s, i) in order:
                t0 = i * 128
                first, last = i == 0, i == s.nt - 1
                k.dma(u[:], s.UT[t0:t0 + 128, 0:896], "sp")
                offs = (-1, 1, -1, 1) if s.isctx else (-1, 1, -64, 64)
                for sl in range(4):
                    r0 = t0 + offs[sl]
                    a0, a1 = max(r0, 0), min(r0 + 128, s.n)
                    k.dma(sh[a0 - r0: a1 - r0, sl * 224:(sl + 1) * 224], s.UT[a0:a1, sl * 224:(sl + 1) * 224],
                          "pool" if sl % 2 else "sp")
                if s.isctx:
                    case = 4 + (1 if first else 0) + (2 if last else 0)
                else:
                    case = (1 if first else 0) + (2 if last else 0)
                mk = smask[case]
                k.tt(t896[:].rr("p (s j) -> p s j", s=4), sh[:].rr("p (s j) -> p s j", s=4),
                     mk[:].un(2).bc((128, 4, 224)), ALU.mult)
                k.tt(t896[:], t896[:], u[:], ALU.subtract)
                k.tt(t896[:], t896[:], mu[:], ALU.mult)
                k.tt(xs[:].rr("p (j s) -> p s j", s=4), t896[:].rr("p (s j) -> p s j", s=4),
                     u[:].rr("p (s j) -> p s j", s=4), ALU.add)
                r_, k_, v_ = xs[:, 0:256], xs[:, 256:512], xs[:, 512:768]
                ck(2)
                p_ = k.ps()
                k.tr(p_[0:32, 0:128], xs[:, 768:800], ident[:])
                k.tr(p_[0:32, 128:256], xs[:, 800:832], ident[:])
                k.tr(p_[0:64, 256:384], xs[:, 832:896], ident[:])
                ck(212)
                k.act(ltw[0:32, :], p_[0:32, 0:128], AF.Tanh)
                ck(213)
                k.cp(lta[0:32, :], p_[0:32, 128:256])
                ck(214)
                k.act(ltg[:], p_[0:64, 256:384], AF.Sigmoid)
                ck(21)
                p_ = k.ps()
                k.mm(p_[:, 0:256], ltw[:], wup[:])
                k.act(lw[:], p_[:, 0:256], AF.Sigmoid)
                k.ts(lw[:], lw[:], -math.exp(-0.5), None, ALU.mult)
                ck(22)
                k.mm(p_[:, 256:512], lta[:], aup[:])
                k.act(alpha[:], p_[:, 256:512], AF.Sigmoid)
                ck(23)
                if d == 0:
                    p2 = k.ps()
                    k.mm(p2[:, 0:256], ltg[:], gup[:])
                    k.cp(gate[:], p2[:, 0:256])
                    k.dma(s.GATE[t0:t0 + 128, :], gate[:], "pool")
                ck(3)
                k.tt(kkr[:], k_, KKb[:], ALU.mult)
                k.tt(t256[:], kkr[:], kkr[:], ALU.mult)
                k.op("dve", "tensor_reduce", s4[:], t256[:].rr("p (h n) -> p h n", h=4), AX.X, ALU.add)
                k.rsqrt(s4b[:], s4[:], 1.0, 1e-12, s4c[:])
                k.tt(kk[:].rr("p (h n) -> p h n", h=4), kkr[:].rr("p (h n) -> p h n", h=4),
                     s4b[:].un(2).bc((128, 4, 64)), ALU.mult)
                k.stt(t256[:], alpha[:], -1.0, KAb[:], ALU.add, ALU.mult)
                k.stt(kd[:], t256[:], 1.0, k_, ALU.add, ALU.mult)
                k.tt(bb[:], kk[:], alpha[:], ALU.mult)
                k.tt(t256[:], r_, kd[:], ALU.mult)
                k.tt(t256[:], t256[:], RKb[:], ALU.mult)
                k.op("dve", "tensor_reduce", s4[:], t256[:].rr("p (h n) -> p h n", h=4), AX.X, ALU.add)
                k.tt(bon[:].rr("p (h n) -> p h n", h=4), v_.rr("p (h n) -> p h n", h=4),
                     s4[:].un(2).bc((128, 4, 64)), ALU.mult)
                k.dma(s.BON[d][t0:t0 + 128, :], bon[:], "pool")
                ck(4)
                pc = k.ps()
                k.mm(pc[:, 0:256], tri[d][:], lw[:])
                k.mm(pc[:, 256:512], ones[:], lw[:])
                k.cp(clw[:], pc[:, 0:256])
                k.act(Pin[:], pc[:, 0:256], AF.Exp)
                k.act(Pinv[:], pc[:, 0:256], AF.Exp, scale=-1.0)
                k.tt(t256[:], clw[:], lw[:], ALU.subtract)
                k.act(Pex[:], t256[:], AF.Exp)
                k.tt(t256b[:], pc[:, 256:512], clw[:], ALU.subtract)
                k.act(Prem[:], t256b[:], AF.Exp)
                pp = k.ps()
                for h in range(4):
                    k.mm(pp[0:64, h:h + 1], lw[:, 64 * h:64 * h + 64], ones[:, 0:1])
                k.act(pcol[:], pp[0:64, 0:4], AF.Exp)
                k.stt(Ah[:], kk[:], -1.0, Pex[:], ALU.mult, ALU.mult)
                k.tt(Rh[:], r_, Pin[:], ALU.mult)
                k.tt(Bc[:], bb[:], Pinv[:], ALU.mult)
                k.tt(Kc[:], kd[:], Pinv[:], ALU.mult)
                k.tt(Bt[:], bb[:], Prem[:], ALU.mult)
                k.tt(Kt[:], kd[:], Prem[:], ALU.mult)
                ck(5)
                orw_t = orw[it % 2]
                for h in range(4):
                    hs = slice(64 * h, 64 * h + 64)
                    ft = FT[h % 2]
                    p_ = k.ps()
                    for j, src in enumerate((Ah, Rh, Bc, Kc)):
                        k.tr(p_[0:64, j * 128:(j + 1) * 128], src[:, hs], ident[:])
                    k.cp(ft[:], p_[0:64, :], eng="act")
                    AT_, RT_, BcT, KcT = (ft[:, j * 128:(j + 1) * 128] for j in range(4))
                    N_, M_ = Nm[h % 2], Mm[h % 2]
                    pn = k.ps()
                    k.mm(pn[:, 0:128], AT_, BcT)
                    k.tt(N_[:], pn[:, 0:128], nmsk[d][:], ALU.mult)
                    pk = k.ps()
                    k.mm(pk[:, 0:256], KcT, ft[:, 0:256])
                    k.tt(AK[h % 2][:], pk[:, 0:256], msk2[d][:], ALU.mult)
                    pb_ = k.ps()
                    k.mm(pb_[:, 0:256], BcT, ft[:, 0:256])
                    k.tt(AB_[h % 2][:], pb_[:, 0:256], msk2[d][:], ALU.mult)
                    k.cp(M_[:], AB_[h % 2][:, 0:128], eng="act")
                    ck(6)
                    Tt = inv_chain(e, N_, M_, pool)
                    ck(7)
                    pw = k.ps()
                    k.mm(pw[0:64, 0:128], Ah[:, hs], Tt[:])
                    wt = WTt[h % 2]
                    k.cp(wt[0:64, :], pw[0:64, 0:128], eng="act")
                    py = k.ps()
                    k.mm(py[:, 0:64], AK[h % 2][:, 0:128], v_[:, hs])
                    ys = Ysb[h % 2]
                    k.cp(ys[:, 0:64], py[:, 0:64])
                    seq_step(HR[d][h], 64, 64, Tt, ys[:, 0:64], wt[0:64, :], RT_, AB_[h % 2][:, 128:256], Bt[:, hs],
                             pcol[:, h:h + 1], orw_t, 64 * h, Ut[h % 2],
                             ext_o=(AK[h % 2][:, 128:256], v_[:, hs]), ext_h=(Kt[:, hs], v_[:, hs]))
                    ck(8)
                k.dma(s.OR[d][t0:t0 + 128, :], orw_t[:], "pool")
                ck(9)

                k.dma(uq[:], s.UQ.rearrange("(c p) t -> p c t", p=128)[:, :, t0:t0 + 132], "sp")
                k.dma(sct[:], s.UT[t0:t0 + 128, 1408:1424], "sp")
                for j in range(5):
                    wj = cw[:, :, j:j + 1].bc((128, 12, 128))
                    if j == 0:
                        k.tt(acc[:], uq[:, :, 0:128], wj, ALU.mult, eng="pool")
                    else:
                        k.tt(tq[:], uq[:, :, j:j + 128], wj, ALU.mult, eng="pool")
                        k.tt(acc[:], acc[:], tq[:], ALU.add, eng="pool")
                k.act(qkv[:], acc[:], AF.Silu)
                ck(10)
                k.tt(sq8[:], qkv[:, 0:8, :], qkv[:, 0:8, :], ALU.mult, eng="pool")
                for hb in range(2):
                    p_ = k.ps()
                    k.mm(p_[:, :], ones[:], sq8[:, 4 * hb:4 * hb + 4, :].rr("p c t -> p (c t)"))
                    k.rsqrt(rs8[:, 4 * hb:4 * hb + 4, :].rr("p c t -> p (c t)"), p_[:, :], 1.0, 1e-6,
                            sq8[:, 4 * hb:4 * hb + 4, :].rr("p c t -> p (c t)"))
                k.stt(qkv[:, 0:4, :], qkv[:, 0:4, :], 128.0 ** -0.5, rs8[:, 0:4, :], ALU.mult, ALU.mult)
                k.tt(qkv[:, 4:8, :], qkv[:, 4:8, :], rs8[:, 4:8, :], ALU.mult)
                for (dst, c0) in ((ktok, 4), (vtok, 8)):
                    p_ = k.ps()
                    for h in range(4):
                        k.tr(p_[:, h * 128:(h + 1) * 128], qkv[:, c0 + h, :], ident[:])
                    k.cp(dst[:].rr("p c t -> p (c t)"), p_[:, :], eng="act")
                ck(11)
                k.act(beta[:], sct[:, 4 * d:4 * d + 4], AF.Sigmoid)
                k.tt(gg[:], sct[:, 8 + 4 * d:12 + 4 * d], dtb, ALU.add)
                k.act(gg[:], gg[:], AF.Exp)
                k.act(gg[:], gg[:], AF.Ln, bias=1.0)
                k.tt(gg[:], gg[:], nega[:], ALU.mult)
                pg = k.ps()
                k.mm(pg[:, 0:4], tri[d][:], gg[:])
                k.mm(pg[:, 4:8], ones[:], gg[:])
                k.cp(gcc[:], pg[:, 0:4])
                k.act(egc[:], pg[:, 0:4], AF.Exp)
                k.tt(bege[:], beta[:], egc[:], ALU.mult)
                k.tt(erem[:], pg[:, 4:8], gcc[:], ALU.subtract)
                k.act(erem[:], erem[:], AF.Exp)
                k.act(etot[:], pg[:, 4:8], AF.Exp)
                k.ts(nbeta[:], beta[:], -1.0, None, ALU.mult)
                ck(12)
                ogd_t = ogd[it % 2]
                for h in range(4):
                    kT, qT = qkv[:, 4 + h, :], qkv[:, h, :]
                    k.ts(trg[:], tri[d][:], gg[:, h:h + 1], None, ALU.mult)
                    pG = k.ps()
                    k.mm(pG[:, 0:128], ones[:], trg[:])
                    k.ts(Dm[:], pG[:, 0:128], gcc[:, h:h + 1], 0.0, ALU.subtract, ALU.max)
                    k.act(Dm[:], Dm[:], AF.Exp, scale=-1.0)
                    k.tt(Dm[:], Dm[:], nmsk[d][:], ALU.mult)
                    k.ts(DTm[:], pG[:, 0:128], gcc[:, h:h + 1], 0.0, ALU.subtract, ALU.min)
                    k.act(DTm[:], DTm[:], AF.Exp)
                    k.tt(DTm[:], DTm[:], msk2[d][:, 128:256], ALU.mult)
                    k.act(EG[:], pG[:, 0:128], AF.Exp)
                    pgr = k.ps()
                    k.mm(pgr[:, 0:128], kT, kT)
                    N_, M_ = Nm[h % 2], Mm[h % 2]
                    k.stt(N_[:], pgr[:, 0:128], nbeta[:, h:h + 1], Dm[:], ALU.mult, ALU.mult)
                    pt = k.ps()
                    k.tr(pt[:, 0:128], N_[:], ident[:])
                    k.cp(M_[:], pt[:, 0:128], eng="act")
                    pa = k.ps()
                    k.mm(pa[:, 0:128], kT, qT)
                    k.tt(ATt[:], pa[:, 0:128], DTm[:], ALU.mult)
                    k.tt(QTt[:], qT, EG[:], ALU.mult)
                    k.ts(kbe[:], ktok[:, h, :], bege[:, h:h + 1], None, ALU.mult)
                    k.ts(vb[:], vtok[:, h, :], beta[:, h:h + 1], None, ALU.mult)
                    k.ts(KBt[:], ktok[:, h, :], erem[:, h:h + 1], None, ALU.mult)
                    Tt = inv_chain(e, N_, M_, pool)
                    pw = k.ps()
                    k.mm(pw[:, 0:128], kbe[:], Tt[:])
                    wt = WTt[h % 2]
                    k.ts(wt[:], pw[:, 0:128], -1.0, None, ALU.mult)
                    seq_step(HG[d][h], 128, 128, Tt, vb[:], wt[:], QTt[:], ATt[:], KBt[:],
                             etot[:, h:h + 1], ogd_t, 128 * h, Ut[h % 2])
                    ck(13)
                k.dma(s.OG[d][t0:t0 + 128, :], ogd_t[:], "pool")
                it += 1
                ck(14)

    def fnet(l, sqs):
        with ExitStack() as e:
            c64 = k.sb(e, (128, 128), F32, "c64"); s64 = k.sb(e, (128, 128), F32, "s64")
            k.dma(c64[:], C["c64bd"]); k.dma(s64[:], C["s64bd"])
            PQ = k.sb(e, (128, 2, 256), F32, "PQ")
            for pr in range(2):
                wf = k.sb(e, (128, 128), F32, "wf")
                k.dma(wf[:], I["fn_wbd"][l][pr])
                p_ = k.ps()
                k.mm(p_[:, 0:128], c64[:], wf[:])
                k.mm(p_[:, 128:256], s64[:], wf[:])
                k.cp(PQ[:, pr, :], p_[:, 0:256])
            gts = [k.sb(e, (128, 2, 128), F32, "gt") for _ in range(2)]
            abt = [k.sb(e, (128, 512), F32, "abt") for _ in range(2)]
            for s in sqs:
                for i in range(s.nt):
                    gt = gts[i % 2]
                    k.dma(gt[:], s.UF.rearrange("(c p) t -> p c t", p=128)[:, :, i * 128:(i + 1) * 128], "sp")
                    p_ = k.ps()
                    for pr in range(2):
                        k.mm(p_[:, pr * 128:(pr + 1) * 128], gt[:, pr, :], PQ[:, pr, 0:128])
                        k.mm(p_[:, 256 + pr * 128:256 + (pr + 1) * 128], gt[:, pr, :], PQ[:, pr, 128:256])
                    ab = abt[i % 2]
                    k.cp(ab[:], p_[:, :], eng="act")
                    k.dma(s.AB[i * 128:(i + 1) * 128, :], ab[:], "pool")
            k.barrier()
            for s in sqs:
                if s.isctx:
                    nk = s.nt
                    cc = k.sb(e, (128, nk, s.n), F32, "cc"); sc_ = k.sb(e, (128, nk, s.n), F32, "scc")
                    k.dma(cc[:], C["cc"].rearrange("k p n -> p k n")); k.dma(sc_[:], C["sc_"].rearrange("k p n -> p k n"))
                    k.ts(sc_[:], sc_[:], -1.0, None, ALU.mult)
                    abc = k.sb(e, (128, nk, 512), F32, "abc")
                    k.dma(abc[:], s.AB.rearrange("(k p) n -> p k n", p=128))
                    for mt in range(nk):
                        p_ = k.ps()
                        for kt in range(nk):
                            k.mm(p_[:, 0:256], cc[:, kt, mt * 128:(mt + 1) * 128], abc[:, kt, 0:256],
                                 start=(kt == 0), stop=False)
                            k.mm(p_[:, 0:256], sc_[:, kt, mt * 128:(mt + 1) * 128], abc[:, kt, 256:512],
                                 start=False, stop=(kt == nk - 1))
                        yf = k.sb(e, (128, 256), F32, "yfc")
                        k.cp(yf[:], p_[:, 0:256])
                        k.dma(s.YF[mt * 128:(mt + 1) * 128, :], yf[:], "pool")
                    continue
                c1 = k.sb(e, (128, 128), F32, "c1"); s1 = k.sb(e, (128, 128), F32, "s1"); ns1 = k.sb(e, (128, 128), F32, "ns1")
                nc1 = k.sb(e, (128, 128), F32, "nc1")
                k.dma(c1[:], C["c1"]); k.dma(s1[:], C["s1"])
                k.ts(ns1[:], s1[:], -1.0, None, ALU.mult)
                k.ts(nc1[:], c1[:], -1.0, None, ALU.mult)
                twc = k.sb(e, (128, T2), F32, "twc"); tws = k.sb(e, (128, T2), F32, "tws")
                k.dma(twc[:], C["twc"]); k.dma(tws[:], C["tws"])
                c3 = k.sb(e, (T2, T2), F32, "c3"); s3 = k.sb(e, (T2, T2), F32, "s3")
                k.dma(c3[:], C["c3"]); k.dma(s3[:], C["s3"])
                ABv = s.AB.rearrange("(a b) n -> a b n", b=T2)
                abs_ = [k.sb(e, (128, 2, 512), F32, "abs") for _ in range(2)]
                yre = k.sb(e, (128, 2, 256), F32, "yre"); yim = k.sb(e, (128, 2, 256), F32, "yim")
                zt = [k.sb(e, (128, 2, 2, 256), F32, "zt") for _ in range(2)]
                tmpz = k.sb(e, (128, 2, 256), F32, "tmpz")
                for b2 in range(T2 // 2):
                    ab = abs_[b2 % 2]
                    k.dma(ab[:], ABv[:, 2 * b2:2 * b2 + 2, :], "sp")
                    pre = k.ps(); pim = k.ps()
                    for j in range(2):
                        k.mm(pre[:, j * 256:(j + 1) * 256], c1[:], ab[:, j, 0:256], start=True, stop=False)
                        k.mm(pre[:, j * 256:(j + 1) * 256], ns1[:], ab[:, j, 256:512], start=False, stop=True)
                        k.mm(pim[:, j * 256:(j + 1) * 256], ns1[:], ab[:, j, 0:256], start=True, stop=False)
                        k.mm(pim[:, j * 256:(j + 1) * 256], nc1[:], ab[:, j, 256:512], start=False, stop=True)
                    k.cp(yre[:].rr("p a d -> p (a d)"), pre[:, :], eng="act")
                    k.cp(yim[:].rr("p a d -> p (a d)"), pim[:, :], eng="act")
                    z = zt[b2 % 2]
                    cb = twc[:, 2 * b2:2 * b2 + 2].un(2).bc((128, 2, 256))
                    sb_ = tws[:, 2 * b2:2 * b2 + 2].un(2).bc((128, 2, 256))
                    k.tt(z[:, :, 0, :], yre[:], cb, ALU.mult)
                    k.tt(tmpz[:], yim[:], sb_, ALU.mult)
                    k.tt(z[:, :, 0, :], z[:, :, 0, :], tmpz[:], ALU.add)
                    k.tt(z[:, :, 1, :], yim[:], cb, ALU.mult)
                    k.tt(tmpz[:], yre[:], sb_, ALU.mult)
                    k.tt(z[:, :, 1, :], z[:, :, 1, :], tmpz[:], ALU.subtract)
                    k.dma(ZS[:, 2 * b2:2 * b2 + 2, :, :], z[:], "pool")
                k.barrier()
                ZSv = ZS.rearrange("a b r d -> b a r d")
                YFv = s.YF.rearrange("(b a) d -> b a d", a=128)
                zl = [k.sb(e, (T2, 2, 2, 256), F32, "zl") for _ in range(2)]
                yo = [k.sb(e, (T2, 2, 256), F32, "yo") for _ in range(2)]
                for a2 in range(64):
                    z = zl[a2 % 2]
                    k.dma(z[:], ZSv[:, 2 * a2:2 * a2 + 2, :, :], "sp")
                    p_ = k.ps()
                    for j in range(2):
                        k.mm(p_[0:T2, j * 256:(j + 1) * 256], c3[:], z[:, j, 0, :], start=True, stop=False)
                        k.mm(p_[0:T2, j * 256:(j + 1) * 256], s3[:], z[:, j, 1, :], start=False, stop=True)
                    y = yo[a2 % 2]
                    k.cp(y[:].rr("p a d -> p (a d)"), p_[0:T2, :], eng="act")
                    k.dma(YFv[:, 2 * a2:2 * a2 + 2, :], y[:], "pool")

    def pass4a(l, sqs):
        with ExitStack() as e:
            wo = k.sb(e, (128, 8, D), BF16, "wo")
            k.dma(wo[:], I["w_out"][l].rearrange("(k p) n -> p k n", p=128), "pool")

            def bcast(name, src, n):
                t = k.sb(e, (128, n), F32, name)
                k.dma(t[:], src.partition_broadcast(128), "sp")
                return t
            lng = bcast("lng", I["rk_vec"][l][3], 256)
            lnb = bcast("lnb", I["rk_vec"][l][4], 256)
            gng = bcast("gng", I["gd_ng"][l][0], 128)

            def W(shape, name, dt=F32):
                return k.sb(e, shape, dt, name)
            xts = [W((128, D), "xt4") for _ in range(2)]
            mixs = [W((128, D), "mix") for _ in range(2)]
            mixb = W((128, D), "mixb", BF16)
            o1s = [W((128, 512), "o1") for _ in range(2)]; o2s = [W((128, 512), "o2") for _ in range(2)]
            zts = [W((128, 512), "zt") for _ in range(2)]
            b1s = [W((128, 256), "b1") for _ in range(2)]; b2s = [W((128, 256), "b2") for _ in range(2)]
            gts = [W((128, 256), "gt") for _ in range(2)]
            oas = [W((128, 256), "oa") for _ in range(2)]; obs = [W((128, 256), "ob") for _ in range(2)]
            s4 = W((128, 4), "p4s4"); s4b = W((128, 4), "p4s4b"); s4c = W((128, 4), "p4s4c"); mean = W((128, 4), "mean")
            mixT = W((128, 8, 128), "mixT", BF16)
            yt = W((128, D), "yt")
            it = 0
            for s in sqs:
                wh = 1 if s.isctx else 0
                xsrc = s.xin if l == 0 else s.X
                for i in range(s.nt):
                    t0 = i * 128
                    b_ = it % 2
                    it += 1
                    xt, mix, o1, o2, zt_, b1, b2_, gt_, oa, ob = (xts[b_], mixs[b_], o1s[b_], o2s[b_], zts[b_], b1s[b_],
                                                                 b2s[b_], gts[b_], oas[b_], obs[b_])
                    k.dma(xt[:], xsrc[t0:t0 + 128, :], "sp")
                    k.dma(oa[:], s.OR[0][t0:t0 + 128, :], "sp")
                    k.dma(ob[:], s.OR[1][t0:t0 + 128, :], "sp")
                    k.dma(b1[:], s.BON[0][t0:t0 + 128, :], "sp")
                    k.dma(b2_[:], s.BON[1][t0:t0 + 128, :], "sp")
                    k.dma(gt_[:], s.GATE[t0:t0 + 128, :], "sp")
                    k.dma(o1[:], s.OG[0][t0:t0 + 128, :], "sp")
                    k.dma(o2[:], s.OG[1][t0:t0 + 128, :], "sp")
                    k.dma(zt_[:], s.UT[t0:t0 + 128, 896:1408], "sp")
                    k.dma(mix[:, 768:1024], s.YF[t0:t0 + 128, :], "sp")
                    o = oa[:]
                    k.tt(o, o, ob[:], ALU.add)
                    ov = o.rr("p (h n) -> p h n", h=4)
                    k.op("dve", "tensor_reduce", s4[:], ov, AX.X, ALU.add)
                    k.ts(mean[:], s4[:], 1.0 / 64, None, ALU.mult)
                    k.tt(ov, ov, mean[:].un(2).bc((128, 4, 64)), ALU.subtract)
                    k.tt(ob[:], o, o, ALU.mult)
                    k.op("dve", "tensor_reduce", s4[:], ob[:].rr("p (h n) -> p h n", h=4), AX.X, ALU.add)
                    k.rsqrt(s4b[:], s4[:], 1.0 / 64, 64e-5, s4c[:])
                    k.tt(ov, ov, s4b[:].un(2).bc((128, 4, 64)), ALU.mult)
                    k.tt(o, o, lng[:], ALU.mult)
                    k.tt(o, o, lnb[:], ALU.add)
                    k.tt(o, o, b1[:], ALU.add)
                    k.tt(o, o, b2_[:], ALU.add)
                    k.tt(mix[:, 0:256], o, gt_[:], ALU.mult)
                    k.tt(o1[:], o1[:], o2[:], ALU.add)
                    k.tt(o2[:], o1[:], o1[:], ALU.mult)
                    k.op("dve", "tensor_reduce", s4[:], o2[:].rr("p (h n) -> p h n", h=4), AX.X, ALU.add)
                    k.rsqrt(s4b[:], s4[:], 1.0 / 128, 1e-6, s4c[:])
                    og = o1[:].rr("p (h n) -> p h n", h=4)
                    k.tt(og, og, s4b[:].un(2).bc((128, 4, 128)), ALU.mult)
                    k.tt(og, og, gng[:].un(1).bc((128, 4, 128)), ALU.mult)
                    k.act(zt_[:], zt_[:], AF.Silu)
                    k.tt(mix[:, 256:768], o1[:], zt_[:], ALU.mult)
                    k.cp(mixb[:], mix[:], eng="pool")
                    p_ = k.ps()
                    pb = p_[:, :].bitcast(BF16)
                    for kc in range(8):
                        k.tr(pb[:, kc * 128:(kc + 1) * 128], mixb[:, kc * 128:(kc + 1) * 128], identb[:])
                    k.cp(mixT[:].rr("p k t -> p (k t)"), pb, eng="act")
                    for hb in range(2):
                        p_ = k.ps()
                        for kc in range(8):
                            k.mm(p_[:, :], mixT[:, kc, :], wo[:, kc, hb * 512:(hb + 1) * 512],
                                 start=(kc == 0), stop=(kc == 7))
                        k.tt(yt[:, hb * 512:(hb + 1) * 512], p_[:, :], M2[wh][:, hb * 512:(hb + 1) * 512], ALU.mult)
                    k.tt(xt[:], xt[:], yt[:], ALU.add)
                    k.dma(s.X[t0:t0 + 128, :], xt[:], "pool")

    def pass4b(l, sqs, lastl):
        with ExitStack() as e:
            w1 = k.sb(e, (128, 8, 4096), BF16, "w1")
            w2 = k.sb(e, (128, 32, D), BF16, "w2")
            s1v = I["mlp_w1"][l].rearrange("(k p) n -> p k n", p=128)
            for kc in range(8):
                for h in range(2):
                    k.dma(w1[:, kc, h * 2048:(h + 1) * 2048], s1v[:, kc, h * 2048:(h + 1) * 2048], "pool")
            s2v = I["mlp_w2"][l].rearrange("(k p) n -> p k n", p=128)
            for kq in range(8):
                k.dma(w2[:, kq * 4:(kq + 1) * 4, :], s2v[:, kq * 4:(kq + 1) * 4, :], "pool")
            fg = None
            if lastl:
                fg = k.sb(e, (128, D), F32, "fg")
                k.dma(fg[:], I["final_g"][0].partition_broadcast(128), "sp")

            def W(shape, name, dt=F32):
                return k.sb(e, shape, dt, name)
            GSM = 256
            xts = [W((128, D), "xt4") for _ in range(2)]
            hT = W((128, 8, GSM), "hT4", BF16)
            aT = W((128, 32, GSM), "aT", BF16)
            yt = W((128, D), "yt")
            tr_ = W((128, GSM), "tr_")
            tmp_pool = (yt, W((128, 1), "ssq4"), W((128, 1), "sq4"), W((128, 1), "rstd4"), W((128, D), "xnb4", BF16))
            for s in sqs:
                wh = 1 if s.isctx else 0
                GS = min(GSM, s.n)
                ntg = GS // 128
                for g0 in range(0, s.n, GS):
                    for j in range(ntg):
                        t0 = g0 + j * 128
                        xt = xts[j]
                        k.dma(xt[:], s.X[t0:t0 + 128, :], "sp")
                        norm_to_hT(e, xt, hT, j * 128, G2[wh], SH2[wh], tmp_pool)
                    for fc in range(32):
                        p_ = k.ps()
                        for kc in range(8):
                            k.mm(p_[:, 0:GS], w1[:, kc, fc * 128:(fc + 1) * 128], hT[:, kc, 0:GS],
                                 start=(kc == 0), stop=(kc == 7))
                        k.act(tr_[:, 0:GS], p_[:, 0:GS], AF.Relu)
                        k.tt(aT[:, fc, 0:GS], tr_[:, 0:GS], tr_[:, 0:GS], ALU.mult)
                    for j in range(ntg):
                        t0 = g0 + j * 128
                        xt = xts[j]
                        for hb in range(2):
                            p_ = k.ps()
                            for fc in range(32):
                                k.mm(p_[:, :], aT[:, fc, j * 128:(j + 1) * 128], w2[:, fc, hb * 512:(hb + 1) * 512],
                                     start=(fc == 0), stop=(fc == 31))
                            k.tt(yt[:, hb * 512:(hb + 1) * 512], p_[:, :], M5[wh][:, hb * 512:(hb + 1) * 512], ALU.mult)
                        k.tt(xt[:], xt[:], yt[:], ALU.add)
                        if lastl:
                            junk, ssq, sq, rstd, _ = tmp_pool
                            k.act(junk[:], xt[:], AF.Square, accum_out=ssq[:])
                            k.rsqrt(rstd[:], ssq[:], 1.0 / D, 1e-6, sq[:])
                            k.stt(yt[:], xt[:], rstd[:, 0:1], fg[:], ALU.mult, ALU.mult)
                            k.dma(OUT[t0:t0 + 128, :], yt[:], "pool")
                        else:
                            k.dma(s.X[t0:t0 + 128, :], xt[:], "pool")

    k.barrier()
    for l in range(L):
        lastl = l == L - 1
        pass0_mod(l)
        k.barrier()
        pass1(l, [SC, SL])
        k.barrier()
        if upto < 2:
            break
        try:
            for d in range(2):
                scan_pass(l, d)
                k.barrier()
        except Cut:
            k.barrier()
            return nc, cst
        if upto < 3:
            break
        sq4 = [SL] if lastl else [SC, SL]
        fnet(l, sq4)
        k.barrier()
        if upto < 4:
            break
        pass4a(l, sq4)
        k.barrier()
        pass4b(l, sq4, lastl)
        k.barrier()
        if upto < 5:
            break
    es.close()
    return nc, cst


def host_layout(inputs, b, L):
    f = lambda a: np.ascontiguousarray(np.asarray(a, dtype=np.float32))
    m = {}
    m["x"] = f(inputs["x"][b])
    m["ctx"] = f(inputs["ctx"][b])
    m["cvec"] = f(np.concatenate([np.asarray(inputs["c"][b]).reshape(8, 128), np.asarray(inputs["c_ctx"]).reshape(8, 128)], 0))
    m["final_g"] = f(np.asarray(inputs["final_g"]).reshape(1, D))
    m["norm_g"] = f(np.concatenate([np.asarray(inputs["norm1_g"]).reshape(L, 8, 128),
                                    np.asarray(inputs["norm2_g"]).reshape(L, 8, 128)], 1))
    m["w_mod"] = f(inputs["w_mod"])
    m["b_mod"] = f(np.asarray(inputs["b_mod"]).reshape(L, 1, 6144))
    perm_r = np.concatenate([np.arange(s_, 896, 4) for s_ in range(4)])
    cols = np.concatenate([perm_r, np.arange(2432, 2944), np.arange(2944, 2960), np.arange(896, 2432),
                           np.arange(2960, 3216)])
    m["w_in"] = f(np.asarray(inputs["w_in"])[:, :, cols])
    m["w_out"] = f(inputs["w_out"])
    m["rk_mu"] = f(np.asarray(inputs["rk_mu"])[:, perm_r].reshape(L, 1, 896))
    m["rk_wup"] = f(np.concatenate([np.asarray(inputs["rk_w_up"]), np.asarray(inputs["rk_w0"])[:, :, None, :]], 2))
    m["rk_aup"] = f(np.concatenate([np.asarray(inputs["rk_a_up"]), np.asarray(inputs["rk_a0"])[:, :, None, :]], 2))
    m["rk_gup"] = f(inputs["rk_g_up"])
    m["rk_vec"] = f(np.stack([np.asarray(inputs["rk_k_k"]), np.asarray(inputs["rk_k_a"]),
                              np.asarray(inputs["rk_r_k"]).reshape(L, 256), np.asarray(inputs["rk_lnx_g"]),
                              np.asarray(inputs["rk_lnx_b"])], 1))
    m["gd_convw"] = f(np.transpose(np.asarray(inputs["gd_conv_w"]), (0, 2, 1)))
    m["gd_sc"] = f(np.concatenate([np.asarray(inputs["gd_a_log"]).reshape(L, 8),
                                   np.asarray(inputs["gd_dt_bias"]).reshape(L, 8)], 1).reshape(L, 1, 16))
    m["gd_ng"] = f(np.asarray(inputs["gd_norm_g"]).reshape(L, 1, 128))
    fw = np.asarray(inputs["fn_w"])
    wbd = np.zeros((L, 2, 128, 128), np.float32)
    for pr in range(2):
        wbd[:, pr, 0:64, 0:64] = fw[:, 2 * pr]
        wbd[:, pr, 64:128, 64:128] = fw[:, 2 * pr + 1]
    m["fn_wbd"] = wbd
    m["mlp_w1"] = f(inputs["mlp_w1"])
    m["mlp_w2"] = f(inputs["mlp_w2"])
    return m


_CACHE = {}


def kernel(**inputs):
    B, T, _ = inputs["x"].shape
    TC = inputs["ctx"].shape[1]
    L = inputs["w_mod"].shape[0]
    key = (T, TC, L)
    if key not in _CACHE:
        _CACHE[key] = build(T, TC, L)
    nc, cst = _CACHE[key]
    in_maps = []
    for core in range(8):
        b = core % B
        m = host_layout(inputs, b, L)
        for k_, v in cst.items():
            m["c_" + k_] = v
        in_maps.append(m)
    res = run_bass_kernel_spmd(nc, in_maps, core_ids=list(range(8)))
    out = np.stack([np.asarray(res.results[b]["out"], dtype=np.float32) for b in range(B)], 0)
    return out
```

```python
import math
from contextlib import ExitStack
import numpy as np
import concourse.bass as bass
import concourse.mybir as mybir
from concourse.bass_utils import run_bass_kernel_spmd

F32 = mybir.dt.float32
BF16 = mybir.dt.bfloat16
ALU = mybir.AluOpType
AF = mybir.ActivationFunctionType
AX = mybir.AxisListType

D = 1024
NTOK = 1424
NFEAT = 1792
RING = 12


class V:
    __slots__ = ("tile", "ap")

    def __init__(self, tile, ap):
        self.tile = tile
        self.ap = ap

    def __getitem__(self, key):
        return V(self.tile, self.ap[key])

    def rr(self, pat, **kw):
        return V(self.tile, self.ap.rearrange(pat, **kw))

    def bc(self, shape):
        return V(self.tile, self.ap.to_broadcast(list(shape)))

    def un(self, axis):
        return V(self.tile, self.ap.unsqueeze(axis))

    def bitcast(self, dt):
        return V(self.tile, self.ap.bitcast(dt))


class Tile:
    __slots__ = ("t", "lw", "rd", "name", "excl")

    def __init__(self, t, name, excl=False):
        self.t = t
        self.lw = None
        self.rd = {}
        self.name = name
        self.excl = excl

    def __getitem__(self, key):
        return V(self, self.t[key])


class Eng:
    def __init__(self, name, be, sem):
        self.name, self.be, self.sem = name, be, sem
        self.cnt = 0
        self.seen = {}


class Ring:
    def __init__(self, sems):
        self.sems = sems
        self.vals = [0] * len(sems)
        self.i = 0


class K:
    def __init__(self, nc, es):
        self.nc = nc
        self.es = es
        self.E = {}
        for name, be in (("pe", nc.tensor), ("act", nc.scalar), ("dve", nc.vector),
                         ("pool", nc.gpsimd), ("sp", nc.sync)):
            sem = es.enter_context(nc.semaphore("s_" + name))
            self.E[name] = Eng(name, be, sem)
        self.dq = {}
        for q in ("sp", "pool"):
            self.dq[q] = Ring([es.enter_context(nc.semaphore("d_%s%d" % (q, i))) for i in range(RING)])
        self.nid = 0
        self.psb = []
        self.psi = 0
        for e in self.E.values():
            nc.sync.sem_clear(e.sem)
        for ring in self.dq.values():
            for sm in ring.sems:
                nc.sync.sem_clear(sm)
        nc.all_engine_barrier()

    def sb(self, es, shape, dt=F32, name=None):
        self.nid += 1
        name = "%s_%d" % (name or "t", self.nid)
        return Tile(es.enter_context(self.nc.sbuf_tensor(name, list(shape), dt)), name)

    def init_psum(self, es):
        for i in range(8):
            self.psb.append(Tile(es.enter_context(self.nc.psum_tensor("ps%d" % i, [128, 512], F32)), "ps%d" % i, True))

    def ps(self):
        t = self.psb[self.psi % 8]
        self.psi += 1
        return t

    def _wait(self, e, tok, raw=False):
        sem, val, key = tok
        if key == e.name and (e.name == "pe" or (e.name in ("dve", "act") and not raw)):
            return
        if e.seen.get(key, 0) >= val:
            return
        e.be.wait_ge(sem, val)
        e.seen[key] = val

    def _deps(self, e, reads, writes):
        for t in reads:
            if t.lw is not None:
                self._wait(e, t.lw, raw=True)
        for t in writes:
            if t.lw is not None:
                self._wait(e, t.lw)
            for tok in t.rd.values():
                self._wait(e, tok)

    def op(self, en, meth, *args, **kw):
        e = self.E[en]
        reads, writes = [], []

        def cv(a, is_out):
            if isinstance(a, V):
                (writes if is_out else reads).append(a.tile)
                return a.ap
            return a
        args2 = [cv(a, i == 0) for i, a in enumerate(args)]
        kw2 = {k_: cv(v, k_ in ("out", "accum_out")) for k_, v in kw.items()}
        writes = writes + [t for t in reads if t.excl]
        reads = [t for t in reads if not t.excl]
        self._deps(e, reads, writes)
        ins = getattr(e.be, meth)(*args2, **kw2)
        e.cnt += 1
        ins.then_inc(e.sem, 1)
        tok = (e.sem, e.cnt, e.name)
        for t in writes:
            t.lw = tok
            t.rd = {}
        for t in reads:
            t.rd[e.name] = tok
        return ins

    def dma(self, out, in_, q="sp"):
        e = self.E[q]
        ring = self.dq[q]
        slot = ring.i % RING
        ring.i += 1
        sem, prev = ring.sems[slot], ring.vals[slot]
        key = "%s_d%d" % (q, slot)
        reads, writes = [], []
        o = out
        i = in_
        if isinstance(out, V):
            writes.append(out.tile)
            o = out.ap
        if isinstance(in_, V):
            reads.append(in_.tile)
            i = in_.ap
        self._deps(e, reads, writes)
        if prev:
            self._wait(e, (sem, prev, key))
        ins = e.be.dma_start(out=o, in_=i)
        ins.then_inc(sem, 16)
        ring.vals[slot] = prev + 16
        tok = (sem, prev + 16, key)
        for t in writes:
            t.lw = tok
            t.rd = {}
        for t in reads:
            t.rd[key] = tok

    def barrier(self):
        toks = [(e.sem, e.cnt, e.name) for e in self.E.values() if e.cnt]
        for q, ring in self.dq.items():
            for s in range(RING):
                if ring.vals[s]:
                    toks.append((ring.sems[s], ring.vals[s], "%s_d%d" % (q, s)))
        for e in self.E.values():
            for tok in toks:
                self._wait(e, tok)

    def mm(self, out, lhsT, rhs, start=True, stop=True):
        return self.op("pe", "matmul", out, lhsT, rhs, start=start, stop=stop)

    def tr(self, out, in_, ident):
        return self.op("pe", "transpose", out, in_, ident)

    def act(self, out, in_, func, bias=0.0, scale=1.0, accum_out=None):
        kw = {}
        if accum_out is not None:
            kw["accum_out"] = accum_out
        return self.op("act", "activation", out, in_, func, bias=bias, scale=scale, **kw)

    def tt(self, out, in0, in1, op, eng="dve"):
        return self.op(eng, "tensor_tensor", out, in0, in1, op)

    def ts(self, out, in0, s1, s2, op0, op1=None, eng="dve"):
        if op1 is None:
            return self.op(eng, "tensor_scalar", out, in0, s1, None, op0)
        return self.op(eng, "tensor_scalar", out, in0, s1, s2, op0, op1)

    def stt(self, out, in0, scalar, in1, op0, op1):
        return self.op("dve", "scalar_tensor_tensor", out, in0, scalar, in1, op0, op1)

    def cp(self, out, in_, eng="dve"):
        if eng == "act":
            return self.act(out, in_, AF.Copy)
        return self.op(eng, "tensor_copy", out, in_)

    def memset(self, out, val, eng="dve"):
        return self.op(eng, "memset", out, val)

    def rsqrt(self, out, in_, scale, eps, tmp):
        self.act(tmp, in_, AF.Sqrt, bias=eps, scale=scale)
        self.op("dve", "reciprocal", out, tmp)


def host_consts(T, TC):
    p = np.arange(128)
    c = {}
    c["ident"] = np.eye(128, dtype=np.float32)
    c["ones"] = np.ones((128, 128), np.float32)
    le = (p[:, None] <= p[None, :]).astype(np.float32)
    lt = (p[:, None] < p[None, :]).astype(np.float32)
    ge = le.T.copy()
    gt = lt.T.copy()
    c["tri"] = np.stack([le, ge])
    c["msk2"] = np.stack([np.concatenate([lt, le], 1), np.concatenate([gt, ge], 1)])
    c["nmsk"] = np.stack([gt, lt])
    c["imsk_ts"] = np.stack([ge, le])
    m = np.ones((8, 128, 4), np.float32)
    lat = np.ones((128, 4), np.float32)
    lat[:, 0] = (p % 64 != 0)
    lat[:, 1] = (p % 64 != 63)
    m[0] = lat
    m[1] = lat; m[1, :, 2] = (p >= 64)
    m[2] = lat; m[2, :, 3] = (p < 64)
    m[3] = lat; m[3, :, 2] = (p >= 64); m[3, :, 3] = (p < 64)
    m[5, :, 0] = (p != 0); m[5, :, 2] = (p != 0)
    m[6, :, 1] = (p != 127); m[6, :, 3] = (p != 127)
    m[7] = m[5] * m[6]
    c["smask"] = m
    k64 = np.arange(64)
    ang = 2 * np.pi * np.outer(k64, k64) / 64.0
    C64, S64 = np.cos(ang), np.sin(ang)
    z = np.zeros((64, 64))
    c["c64bd"] = np.block([[C64, z], [z, C64]]).astype(np.float32)
    c["s64bd"] = np.block([[S64, z], [z, S64]]).astype(np.float32)
    T2 = T // 128
    a1 = 2 * np.pi * np.outer(p, p) / 128.0
    c["c1"] = np.cos(a1).astype(np.float32)
    c["s1"] = np.sin(a1).astype(np.float32)
    tw = 2 * np.pi * np.outer(p, np.arange(T2)) / float(T)
    c["twc"] = np.cos(tw).astype(np.float32)
    c["tws"] = np.sin(tw).astype(np.float32)
    a3 = 2 * np.pi * np.outer(np.arange(T2), np.arange(T2)) / float(T2)
    nrm = 1.0 / math.sqrt(T * 64.0)
    c["c3"] = (np.cos(a3) * nrm).astype(np.float32)
    c["s3"] = (np.sin(a3) * nrm).astype(np.float32)
    tc = np.arange(TC)
    ac = 2 * np.pi * np.outer(tc, tc) / float(TC)
    nrc = 1.0 / math.sqrt(TC * 64.0)
    c["cc"] = (np.cos(ac) * nrc).astype(np.float32).reshape(TC // 128, 128, TC)
    c["sc_"] = (np.sin(ac) * nrc).astype(np.float32).reshape(TC // 128, 128, TC)
    return c


PARAMS = [
    ("norm_g", (16, 128)),
    ("w_mod", (1024, 6144)),
    ("b_mod", (1, 6144)),
    ("w_in", (1024, 3216)),
    ("w_out", (1024, 1024)),
    ("rk_mu", (1, 896)),
    ("rk_wup", (2, 33, 256)),
    ("rk_aup", (2, 33, 256)),
    ("rk_gup", (64, 256)),
    ("rk_vec", (5, 256)),
    ("gd_convw", (1536, 5)),
    ("gd_sc", (1, 16)),
    ("gd_ng", (1, 128)),
    ("fn_wbd", (2, 128, 128)),
    ("mlp_w1", (1024, 4096)),
    ("mlp_w2", (4096, 1024)),
]


class Cut(Exception):
    pass


def build(T, TC, L, debug=(), upto=99, cut=0):
    nc = bass.Bass("TRN2", target_bir_lowering=False)
    NT, NTC = T // 128, TC // 128
    T2 = T // 128
    cst = host_consts(T, TC)

    def din(name, shape, dt=F32):
        return nc.dram_tensor(name, list(shape), dt, kind="ExternalInput").ap()

    def dscr(name, shape, dt=F32):
        kind = "ExternalOutput" if name in debug else "Internal"
        return nc.dram_tensor(name, list(shape), dt, kind=kind).ap()

    I = {}
    I["x"] = din("x", (T, D))
    I["ctx"] = din("ctx", (TC, D))
    I["cvec"] = din("cvec", (16, 128))
    I["final_g"] = din("final_g", (1, D))
    for name, shp in PARAMS:
        I[name] = din(name, (L,) + shp)
    C = {k_: din("c_" + k_, v.shape) for k_, v in cst.items()}
    OUT = nc.dram_tensor("out", [T, D], F32, kind="ExternalOutput").ap()

    class Seq:
        pass
    seqs = []
    for nm, n, xin in (("c", TC, I["ctx"]), ("l", T, I["x"])):
        s = Seq()
        s.nm, s.n, s.nt, s.xin = nm, n, n // 128, xin
        s.isctx = nm == "c"
        s.X = dscr("X_" + nm, (n, D))
        s.UT = dscr("UT_" + nm, (n, NTOK))
        s.UQ = dscr("UQ_" + nm, (1536, n + 4))
        s.UF = dscr("UF_" + nm, (256, n))
        s.OR = [dscr("OR%d_%s" % (d, nm), (n, 256)) for d in range(2)]
        s.BON = [dscr("BON%d_%s" % (d, nm), (n, 256)) for d in range(2)]
        s.GATE = dscr("GATE_" + nm, (n, 256))
        s.OG = [dscr("OG%d_%s" % (d, nm), (n, 512)) for d in range(2)]
        s.AB = dscr("AB_" + nm, (n, 512))
        s.YF = dscr("YF_" + nm, (n, 256))
        seqs.append(s)
    SC, SL = seqs
    ZS = dscr("ZS", (128, T2, 2, 256))

    es = ExitStack()
    k = K(nc, es)
    k.init_psum(es)

    def dbgout(name, view, shape):
        if name in debug:
            t = nc.dram_tensor(name, list(shape), F32, kind="ExternalOutput").ap()
            k.dma(t, view, "sp")

    def cload(name, shape, src=None, q="sp"):
        t = k.sb(es, shape, F32, name)
        k.dma(t[:], C[name] if src is None else src, q)
        return t
    ident = cload("ident", (128, 128))
    ones = cload("ones", (128, 128))
    tri = [cload("tri", (128, 128), C["tri"][d]) for d in range(2)]
    msk2 = [cload("msk2", (128, 256), C["msk2"][d]) for d in range(2)]
    nmsk = [cload("nmsk", (128, 128), C["nmsk"][d]) for d in range(2)]
    imsk_ts = [cload("imsk_ts", (128, 128), C["imsk_ts"][d]) for d in range(2)]
    smask = [cload("smask", (128, 4), C["smask"][i]) for i in range(8)]
    identb = k.sb(es, (128, 128), BF16, "identb")
    k.dma(identb[:], C["ident"], "pool")
    zeros = k.sb(es, (128, 512), F32, "zeros")
    k.memset(zeros[:], 0.0)
    warm = k.sb(es, (128, 8), F32, "warm")
    k.act(warm[:], zeros[:, 0:8], AF.Silu)
    for s in seqs:
        for c0 in (0, s.n + 2):
            k.dma(s.UQ.rearrange("(c p) t -> p c t", p=128)[:, :, c0:c0 + 2],
                  zeros[:, 0:24].rr("p (c t) -> p c t", t=2), "sp")
    G1 = [k.sb(es, (128, 8), F32, "G1") for _ in range(2)]
    SH1 = [k.sb(es, (128, 8), F32, "SH1") for _ in range(2)]
    G2 = [k.sb(es, (128, 8), F32, "G2") for _ in range(2)]
    SH2 = [k.sb(es, (128, 8), F32, "SH2") for _ in range(2)]
    M2 = [k.sb(es, (128, D), F32, "M2") for _ in range(2)]
    M5 = [k.sb(es, (128, D), F32, "M5") for _ in range(2)]
    scT = k.sb(es, (128, 16), F32, "scT")
    with ExitStack() as e0:
        cv = k.sb(e0, (16, 128), F32, "cv")
        k.dma(cv[:], I["cvec"])
        p_ = k.ps()
        k.tr(p_[:, 0:16], cv[:], ident[0:16, 0:16])
        k.act(scT[:], p_[:, 0:16], AF.Silu)
    dbgout("D_scT", scT[:], (128, 16))

    HR = [[k.sb(es, (64, 64), F32, "HR") for h in range(4)] for d in range(2)]
    HG = [[k.sb(es, (128, 128), F32, "HG") for h in range(4)] for d in range(2)]

    def pass0_mod(l):
        with ExitStack() as e:
            mrow = [k.sb(e, (1, 6144), F32, "mrow") for _ in range(2)]
            bm = k.sb(e, (1, 6144), F32, "bm")
            raw = k.sb(e, (1, 6144), F32, "raw")
            k.dma(bm[:], I["b_mod"][l])
            wmb = [k.sb(e, (128, 8, 512), F32, "wmb") for _ in range(2)]
            for blk in range(12):
                w = wmb[blk % 2]
                k.dma(w[:], I["w_mod"][l].rearrange("(k p) n -> p k n", p=128)[:, :, blk * 512:(blk + 1) * 512],
                      "sp" if blk % 2 == 0 else "pool")
                if l == 0 and blk < 2:
                    dbgout("D_w%d" % blk, w[:], (128, 8, 512))
                for wh in range(2):
                    p_ = k.ps()
                    for kc in range(8):
                        col = (0 if wh == 0 else 8) + kc
                        k.mm(p_[0:1, :], scT[:, col:col + 1], w[:, kc, :], start=(kc == 0), stop=(kc == 7))
                    if wh == 0 and l == 0 and "D_raw" in debug:
                        k.cp(raw[:, blk * 512:(blk + 1) * 512], p_[0:1, :])
                    k.tt(mrow[wh][:, blk * 512:(blk + 1) * 512], p_[0:1, :], bm[:, blk * 512:(blk + 1) * 512], ALU.add)
            if l == 0:
                dbgout("D_mrow", mrow[0][:], (1, 6144))
                dbgout("D_raw", raw[:], (1, 6144))
                dbgout("D_bm", bm[:], (1, 6144))
            ng = k.sb(e, (16, 128), F32, "ng")
            k.dma(ng[:], I["norm_g"][l])
            gcol = k.sb(e, (128, 16), F32, "gcol")
            p_ = k.ps()
            k.tr(p_[:, 0:16], ng[:], ident[0:16, 0:16])
            k.cp(gcol[:], p_[:, 0:16])
            for wh in range(2):
                mc = k.sb(e, (128, 48), F32, "mc")
                p_ = k.ps()
                for j in range(48):
                    k.mm(p_[:, j:j + 1], mrow[wh][:, j * 128:(j + 1) * 128], ones[0:1, 0:1])
                k.cp(mc[:], p_[:, 0:48])
                k.stt(G1[wh][:], mc[:, 8:16], 1.0, gcol[:, 0:8], ALU.add, ALU.mult)
                k.stt(G2[wh][:], mc[:, 32:40], 1.0, gcol[:, 8:16], ALU.add, ALU.mult)
                k.cp(SH1[wh][:], mc[:, 0:8])
                k.cp(SH2[wh][:], mc[:, 24:32])
                if l == 0 and wh == 0:
                    dbgout("D_mc", mc[:], (128, 48))
                    dbgout("D_G1", G1[0][:], (128, 8))
                for dst, off in ((M2[wh], 2048), (M5[wh], 5120)):
                    for hb in range(2):
                        p_ = k.ps()
                        k.mm(p_[:, :], ones[0:1, :], mrow[wh][:, off + hb * 512: off + (hb + 1) * 512])
                        k.cp(dst[:, hb * 512:(hb + 1) * 512], p_[:, :], eng="act")

    dbgdone = []

    def norm_to_hT(e, xt, hT, col0, Gc, SHc, tmp_pool):
        junk, ssq, sq, rstd, xnb = tmp_pool
        k.act(junk[:], xt[:], AF.Square, accum_out=ssq[:])
        k.rsqrt(rstd[:], ssq[:], 1.0 / D, 1e-6, sq[:])
        k.ts(xnb[:], xt[:], rstd[:, 0:1], None, ALU.mult)
        if "D_ssq" in debug and not dbgdone:
            dbgdone.append(1)
            dbgout("D_ssq", ssq[:], (128, 1))
            dbgout("D_rstd", rstd[:], (128, 1))
            dbgout("D_junk", junk[:], (128, D))
        p_ = k.ps()
        pb = p_[:, :].bitcast(BF16)
        for kc in range(8):
            k.tr(pb[:, kc * 128:(kc + 1) * 128], xnb[:, kc * 128:(kc + 1) * 128], identb[:])
        for kc in range(8):
            k.act(hT[:, kc, col0:col0 + 128], pb[:, kc * 128:(kc + 1) * 128], AF.Identity,
                  bias=SHc[:, kc:kc + 1], scale=Gc[:, kc:kc + 1])

    def pass1(l, sqs):
        with ExitStack() as e:
            win = k.sb(e, (128, 8, 3216), BF16, "win")
            wsrc = I["w_in"][l].rearrange("(k p) n -> p k n", p=128)
            for kc in range(8):
                for h in range(2):
                    k.dma(win[:, kc, h * 1608:(h + 1) * 1608], wsrc[:, kc, h * 1608:(h + 1) * 1608], "pool")
            hT = [k.sb(e, (128, 8, 512), BF16, "hT") for _ in range(2)]
            xts = [k.sb(e, (128, D), F32, "xt") for _ in range(2)]
            tmp_pool = (k.sb(e, (128, D), F32, "junk"), k.sb(e, (128, 1), F32, "ssq"), k.sb(e, (128, 1), F32, "sq"),
                        k.sb(e, (128, 1), F32, "rstd"), k.sb(e, (128, D), BF16, "xnb"))
            ust = [k.sb(e, (128, NTOK), F32, "ust") for _ in range(2)]
            fst = [k.sb(e, (128, 512), F32, "fst") for _ in range(3)]
            gi = 0
            for s in sqs:
                wh = 1 if s.isctx else 0
                xsrc = s.xin if l == 0 else s.X
                GS = min(512, s.n)
                ntg = GS // 128
                for g0 in range(0, s.n, GS):
                    h_ = hT[gi % 2]
                    gi += 1
                    for j in range(ntg):
                        xt = xts[j % 2]
                        k.dma(xt[:], xsrc[g0 + j * 128: g0 + (j + 1) * 128, :], "sp")
                        norm_to_hT(e, xt, h_, j * 128, G1[wh], SH1[wh], tmp_pool)
                    for j in range(ntg):
                        u = ust[j % 2]
                        for (c0, c1) in ((0, 512), (512, 1024), (1024, NTOK)):
                            p_ = k.ps()
                            for kc in range(8):
                                k.mm(p_[:, 0:c1 - c0], h_[:, kc, j * 128:(j + 1) * 128], win[:, kc, c0:c1],
                                     start=(kc == 0), stop=(kc == 7))
                            k.cp(u[:, c0:c1], p_[:, 0:c1 - c0], eng="act" if c0 == 512 else "dve")
                        k.dma(s.UT[g0 + j * 128: g0 + (j + 1) * 128, :], u[:], "pool")
                    for ct in range(14):
                        p_ = k.ps()
                        for kc in range(8):
                            k.mm(p_[:, 0:GS], win[:, kc, NTOK + ct * 128: NTOK + (ct + 1) * 128], h_[:, kc, 0:GS],
                                 start=(kc == 0), stop=(kc == 7))
                        f = fst[ct % 3]
                        k.cp(f[:, 0:GS], p_[:, 0:GS], eng="act" if ct % 2 else "dve")
                        if ct < 12:
                            k.dma(s.UQ[ct * 128:(ct + 1) * 128, 2 + g0: 2 + g0 + GS], f[:, 0:GS], "sp")
                        else:
                            k.dma(s.UF[(ct - 12) * 128:(ct - 11) * 128, g0: g0 + GS], f[:, 0:GS], "sp")

    def inv_chain(e, N, M, pool):
        R = pool["R"]
        k.tt(R[:], M[:], ident[:], ALU.add)
        Nk, Mk = N, M
        for lev in range(7):
            if lev == 0:
                pn = k.ps()
                k.mm(pn[:, 0:128], Mk[:], Nk[:])
                pm = k.ps()
                k.mm(pm[:, 0:128], Nk[:], Mk[:])
                Nn, Mn = pool["N"][0], pool["M"][0]
                k.cp(Nn[:], pn[:, 0:128], eng="act")
                k.cp(Mn[:], pm[:, 0:128], eng="dve")
                Nk, Mk = Nn, Mn
                continue
            last = lev == 6
            pr = k.ps()
            if not last:
                rm = pool["RM"]
                k.cp(rm[:, 0:128], R[:], eng="act")
                k.cp(rm[:, 128:256], Mk[:], eng="act")
                k.mm(pr[:, 0:256], Nk[:], rm[:])
                pn = k.ps()
                k.mm(pn[:, 0:128], Mk[:], Nk[:])
                Nn, Mn = pool["N"][lev % 2], pool["M"][lev % 2]
                k.tt(R[:], R[:], pr[:, 0:128], ALU.add)
                k.cp(Mn[:], pr[:, 128:256], eng="act")
                k.cp(Nn[:], pn[:, 0:128], eng="dve")
                Nk, Mk = Nn, Mn
            else:
                k.mm(pr[:, 0:128], Nk[:], R[:])
                Tt = pool["Tt"]
                k.tt(Tt[:], R[:], pr[:, 0:128], ALU.add)
        return pool["Tt"]

    def seq_step(H, Kd, Vd, Tt, Yrhs, WT, QT, AT, KB, dec, otile, ocol, Ut, ext_o=None, ext_h=None):
        pu = k.ps()
        k.mm(pu[:, 0:Vd], Tt[:], Yrhs, start=True, stop=False)
        k.mm(pu[:, 0:Vd], WT, H[:], start=False, stop=True)
        k.cp(Ut[:, 0:Vd], pu[:, 0:Vd], eng="act")
        po = k.ps()
        k.mm(po[:, 0:Vd], QT, H[:], start=True, stop=False)
        k.mm(po[:, 0:Vd], AT, Ut[:, 0:Vd], start=False, stop=(ext_o is None))
        if ext_o is not None:
            k.mm(po[:, 0:Vd], ext_o[0], ext_o[1], start=False, stop=True)
        k.cp(otile[:, ocol:ocol + Vd], po[:, 0:Vd], eng="act")
        ph = k.ps()
        k.mm(ph[0:Kd, 0:Vd], KB, Ut[:, 0:Vd], start=True, stop=(ext_h is None))
        if ext_h is not None:
            k.mm(ph[0:Kd, 0:Vd], ext_h[0], ext_h[1], start=False, stop=True)
        k.stt(H[:], H[:], dec, ph[0:Kd, 0:Vd], ALU.mult, ALU.add)

    def ck(n):
        if cut == n:
            raise Cut()

    def scan_pass(l, d):
        with ExitStack() as e:
            def bcast(name, src, n):
                t = k.sb(e, (128, n), F32, name)
                k.dma(t[:], src.partition_broadcast(128), "sp")
                return t
            mu = bcast("mu", I["rk_mu"][l][0], 896)
            rv = [bcast("rv", I["rk_vec"][l][j], 256) for j in range(3)]
            KKb, KAb, RKb = rv
            gsc = bcast("gsc", I["gd_sc"][l][0], 16)
            nega = k.sb(e, (128, 4), F32, "nega")
            k.act(nega[:], gsc[:, 4 * d:4 * d + 4], AF.Exp)
            k.ts(nega[:], nega[:], -1.0, None, ALU.mult)
            dtb = gsc[:, 8 + 4 * d: 12 + 4 * d]
            wup = k.sb(e, (33, 256), F32, "wup")
            k.dma(wup[:], I["rk_wup"][l][d])
            aup = k.sb(e, (33, 256), F32, "aup")
            k.dma(aup[:], I["rk_aup"][l][d])
            gup = k.sb(e, (64, 256), F32, "gup")
            k.dma(gup[:], I["rk_gup"][l])
            cw = k.sb(e, (128, 12, 5), F32, "cw")
            k.dma(cw[:], I["gd_convw"][l].rearrange("(c p) j -> p c j", p=128))
            def W(shape, name, dt=F32):
                return k.sb(e, shape, dt, name)
            u = W((128, 896), "u"); sh = W((128, 896), "sh"); xs = W((128, 896), "xs"); t896 = W((128, 896), "t896")
            k.memset(sh[:], 0.0)
            ltw = W((33, 128), "ltw"); lta = W((33, 128), "lta"); ltg = W((64, 128), "ltg")
            k.memset(ltw[:], 1.0)
            k.memset(lta[:], 1.0)
            lw = W((128, 256), "lw"); alpha = W((128, 256), "alpha"); gate = W((128, 256), "gate")
            kkr = W((128, 256), "kkr"); t256 = W((128, 256), "t256"); t256b = W((128, 256), "t256b")
            s4 = W((128, 4), "s4"); s4b = W((128, 4), "s4b"); s4c = W((128, 4), "s4c")
            kk = W((128, 256), "kk"); kd = W((128, 256), "kd"); bb = W((128, 256), "bb"); bon = W((128, 256), "bon")
            clw = W((128, 256), "clw"); Pin = W((128, 256), "Pin"); Pex = W((128, 256), "Pex")
            Pinv = W((128, 256), "Pinv"); Prem = W((128, 256), "Prem")
            Ah = W((128, 256), "Ah"); Rh = W((128, 256), "Rh"); Bc = W((128, 256), "Bc"); Kc = W((128, 256), "Kc")
            Bt = W((128, 256), "Bt"); Kt = W((128, 256), "Kt")
            pcol = W((64, 4), "pcol")
            FT = [W((64, 512), "FT") for _ in range(2)]
            Nm = [W((128, 128), "Nm") for _ in range(2)]; Mm = [W((128, 128), "Mm") for _ in range(2)]
            AK = [W((128, 256), "AK") for _ in range(2)]; AB_ = [W((128, 256), "ABm") for _ in range(2)]
            pool = {"R": W((128, 128), "R"), "N": [W((128, 128), "Nn") for _ in range(2)],
                    "M": [W((128, 128), "Mn") for _ in range(2)], "RM": W((128, 256), "RM"), "Tt": W((128, 128), "Tt")}
            WTt = [W((128, 128), "WT") for _ in range(2)]
            Ysb = [W((128, 128), "Ysb") for _ in range(2)]
            Ut = [W((128, 128), "Ut") for _ in range(2)]
            orw = [W((128, 256), "orw") for _ in range(2)]
            ogd = [W((128, 512), "ogd") for _ in range(2)]
            uq = W((128, 12, 132), "uq"); acc = W((128, 12, 128), "acc"); tq = W((128, 12, 128), "tq")
            qkv = W((128, 12, 128), "qkv"); sq8 = W((128, 8, 128), "sq8"); rs8 = W((128, 8, 128), "rs8")
            ktok = W((128, 4, 128), "ktok"); vtok = W((128, 4, 128), "vtok")
            sct = W((128, 16), "sct"); beta = W((128, 4), "beta"); gg = W((128, 4), "gg"); gcc = W((128, 4), "gcc")
            ngc = W((128, 4), "ngc"); egc = W((128, 4), "egc"); bege = W((128, 4), "bege"); erem = W((128, 4), "erem")
            etot = W((128, 4), "etot"); nbeta = W((128, 4), "nbeta")
            trg = W((128, 128), "trg"); Dm = W((128, 128), "Dm"); DTm = W((128, 128), "DTm"); EG = W((128, 128), "EG")
            kbe = W((128, 128), "kbe"); vb = W((128, 128), "vb"); KBt = W((128, 128), "KBt"); QTt = W((128, 128), "QTt")
            ATt = W((128, 128), "ATt")

            for h in range(4):
                k.memset(HR[d][h][:], 0.0)
                k.memset(HG[d][h][:], 0.0)

            order = []
            for s in (SC, SL):
                tl = list(range(s.nt))
                if d == 1:
                    tl = tl[::-1]
                order += [(s, i) for i in tl]
            it = 0
            ck(1)
            for (s, i) in order:
                t0 = i * 128
                first, last = i == 0, i == s.nt - 1
                k.dma(u[:], s.UT[t0:t0 + 128, 0:896], "sp")
                offs = (-1, 1, -1, 1) if s.isctx else (-1, 1, -64, 64)
                for sl in range(4):
                    r0 = t0 + offs[sl]
                    a0, a1 = max(r0, 0), min(r0 + 128, s.n)
                    k.dma(sh[a0 - r0: a1 - r0, sl * 224:(sl + 1) * 224], s.UT[a0:a1, sl * 224:(sl + 1) * 224],
                          "pool" if sl % 2 else "sp")
                if s.isctx:
                    case = 4 + (1 if first else 0) + (2 if last else 0)
                else:
                    case = (1 if first else 0) + (2 if last else 0)
                mk = smask[case]
                k.tt(t896[:].rr("p (s j) -> p s j", s=4), sh[:].rr("p (s j) -> p s j", s=4),
                     mk[:].un(2).bc((128, 4, 224)), ALU.mult)
                k.tt(t896[:], t896[:], u[:], ALU.subtract)
                k.tt(t896[:], t896[:], mu[:], ALU.mult)
                k.tt(xs[:].rr("p (j s) -> p s j", s=4), t896[:].rr("p (s j) -> p s j", s=4),
                     u[:].rr("p (s j) -> p s j", s=4), ALU.add)
                r_, k_, v_ = xs[:, 0:256], xs[:, 256:512], xs[:, 512:768]
                ck(2)
                p_ = k.ps()
                k.tr(p_[0:32, 0:128], xs[:, 768:800], ident[:])
                k.tr(p_[0:32, 128:256], xs[:, 800:832], ident[:])
                k.tr(p_[0:64, 256:384], xs[:, 832:896], ident[:])
                ck(212)
                k.act(ltw[0:32, :], p_[0:32, 0:128], AF.Tanh)
                ck(213)
                k.cp(lta[0:32, :], p_[0:32, 128:256])
                ck(214)
                k.act(ltg[:], p_[0:64, 256:384], AF.Sigmoid)
                ck(21)
                p_ = k.ps()
                k.mm(p_[:, 0:256], ltw[:], wup[:])
                k.act(lw[:], p_[:, 0:256], AF.Sigmoid)
                k.ts(lw[:], lw[:], -math.exp(-0.5), None, ALU.mult)
                ck(22)
                k.mm(p_[:, 256:512], lta[:], aup[:])
                k.act(alpha[:], p_[:, 256:512], AF.Sigmoid)
                ck(23)
                if d == 0:
                    p2 = k.ps()
                    k.mm(p2[:, 0:256], ltg[:], gup[:])
                    k.cp(gate[:], p2[:, 0:256])
                    k.dma(s.GATE[t0:t0 + 128, :], gate[:], "pool")
                ck(3)
                k.tt(kkr[:], k_, KKb[:], ALU.mult)
                k.tt(t256[:], kkr[:], kkr[:], ALU.mult)
                k.op("dve", "tensor_reduce", s4[:], t256[:].rr("p (h n) -> p h n", h=4), AX.X, ALU.add)
                k.rsqrt(s4b[:], s4[:], 1.0, 1e-12, s4c[:])
                k.tt(kk[:].rr("p (h n) -> p h n", h=4), kkr[:].rr("p (h n) -> p h n", h=4),
                     s4b[:].un(2).bc((128, 4, 64)), ALU.mult)
                k.stt(t256[:], alpha[:], -1.0, KAb[:], ALU.add, ALU.mult)
                k.stt(kd[:], t256[:], 1.0, k_, ALU.add, ALU.mult)
                k.tt(bb[:], kk[:], alpha[:], ALU.mult)
                k.tt(t256[:], r_, kd[:], ALU.mult)
                k.tt(t256[:], t256[:], RKb[:], ALU.mult)
                k.op("dve", "tensor_reduce", s4[:], t256[:].rr("p (h n) -> p h n", h=4), AX.X, ALU.add)
                k.tt(bon[:].rr("p (h n) -> p h n", h=4), v_.rr("p (h n) -> p h n", h=4),
                     s4[:].un(2).bc((128, 4, 64)), ALU.mult)
                k.dma(s.BON[d][t0:t0 + 128, :], bon[:], "pool")
                ck(4)
                pc = k.ps()
                k.mm(pc[:, 0:256], tri[d][:], lw[:])
                k.mm(pc[:, 256:512], ones[:], lw[:])
                k.cp(clw[:], pc[:, 0:256])
                k.act(Pin[:], pc[:, 0:256], AF.Exp)
                k.act(Pinv[:], pc[:, 0:256], AF.Exp, scale=-1.0)
                k.tt(t256[:], clw[:], lw[:], ALU.subtract)
                k.act(Pex[:], t256[:], AF.Exp)
                k.tt(t256b[:], pc[:, 256:512], clw[:], ALU.subtract)
                k.act(Prem[:], t256b[:], AF.Exp)
                pp = k.ps()
                for h in range(4):
                    k.mm(pp[0:64, h:h + 1], lw[:, 64 * h:64 * h + 64], ones[:, 0:1])
                k.act(pcol[:], pp[0:64, 0:4], AF.Exp)
                k.stt(Ah[:], kk[:], -1.0, Pex[:], ALU.mult, ALU.mult)
                k.tt(Rh[:], r_, Pin[:], ALU.mult)
                k.tt(Bc[:], bb[:], Pinv[:], ALU.mult)
                k.tt(Kc[:], kd[:], Pinv[:], ALU.mult)
                k.tt(Bt[:], bb[:], Prem[:], ALU.mult)
                k.tt(Kt[:], kd[:], Prem[:], ALU.mult)
                ck(5)
                orw_t = orw[it % 2]
                for h in range(4):
                    hs = slice(64 * h, 64 * h + 64)
                    ft = FT[h % 2]
                    p_ = k.ps()
                    for j, src in enumerate((Ah, Rh, Bc, Kc)):
                        k.tr(p_[0:64, j * 128:(j + 1) * 128], src[:, hs], ident[:])
                    k.cp(ft[:], p_[0:64, :], eng="act")
                    AT_, RT_, BcT, KcT = (ft[:, j * 128:(j + 1) * 128] for j in range(4))
                    N_, M_ = Nm[h % 2], Mm[h % 2]
                    pn = k.ps()
                    k.mm(pn[:, 0:128], AT_, BcT)
                    k.tt(N_[:], pn[:, 0:128], nmsk[d][:], ALU.mult)
                    pk = k.ps()
                    k.mm(pk[:, 0:256], KcT, ft[:, 0:256])
                    k.tt(AK[h % 2][:], pk[:, 0:256], msk2[d][:], ALU.mult)
                    pb_ = k.ps()
                    k.mm(pb_[:, 0:256], BcT, ft[:, 0:256])
                    k.tt(AB_[h % 2][:], pb_[:, 0:256], msk2[d][:], ALU.mult)
                    k.cp(M_[:], AB_[h % 2][:, 0:128], eng="act")
                    ck(6)
                    Tt = inv_chain(e, N_, M_, pool)
                    ck(7)
                    pw = k.ps()
                    k.mm(pw[0:64, 0:128], Ah[:, hs], Tt[:])
                    wt = WTt[h % 2]
                    k.cp(wt[0:64, :], pw[0:64, 0:128], eng="act")
                    py = k.ps()
                    k.mm(py[:, 0:64], AK[h % 2][:, 0:128], v_[:, hs])
                    ys = Ysb[h % 2]
                    k.cp(ys[:, 0:64], py[:, 0:64])
                    seq_step(HR[d][h], 64, 64, Tt, ys[:, 0:64], wt[0:64, :], RT_, AB_[h % 2][:, 128:256], Bt[:, hs],
                             pcol[:, h:h + 1], orw_t, 64 * h, Ut[h % 2],
                             ext_o=(AK[h % 2][:, 128:256], v_[:, hs]), ext_h=(Kt[:, hs], v_[:, hs]))
                    ck(8)
                k.dma(s.OR[d][t0:t0 + 128, :], orw_t[:], "pool")
                ck(9)

                k.dma(uq[:], s.UQ.rearrange("(c p) t -> p c t", p=128)[:, :, t0:t0 + 132], "sp")
                k.dma(sct[:], s.UT[t0:t0 + 128, 1408:1424], "sp")
                for j in range(5):
                    wj = cw[:, :, j:j + 1].bc((128, 12, 128))
                    if j == 0:
                        k.tt(acc[:], uq[:, :, 0:128], wj, ALU.mult, eng="pool")
                    else:
                        k.tt(tq[:], uq[:, :, j:j + 128], wj, ALU.mult, eng="pool")
                        k.tt(acc[:], acc[:], tq[:], ALU.add, eng="pool")
                k.act(qkv[:], acc[:], AF.Silu)
                ck(10)
                k.tt(sq8[:], qkv[:, 0:8, :], qkv[:, 0:8, :], ALU.mult, eng="pool")
                for hb in range(2):
                    p_ = k.ps()
                    k.mm(p_[:, :], ones[:], sq8[:, 4 * hb:4 * hb + 4, :].rr("p c t -> p (c t)"))
                    k.rsqrt(rs8[:, 4 * hb:4 * hb + 4, :].rr("p c t -> p (c t)"), p_[:, :], 1.0, 1e-6,
                            sq8[:, 4 * hb:4 * hb + 4, :].rr("p c t -> p (c t)"))
                k.stt(qkv[:, 0:4, :], qkv[:, 0:4, :], 128.0 ** -0.5, rs8[:, 0:4, :], ALU.mult, ALU.mult)
                k.tt(qkv[:, 4:8, :], qkv[:, 4:8, :], rs8[:, 4:8, :], ALU.mult)
                for (dst, c0) in ((ktok, 4), (vtok, 8)):
                    p_ = k.ps()
                    for h in range(4):
                        k.tr(p_[:, h * 128:(h + 1) * 128], qkv[:, c0 + h, :], ident[:])
                    k.cp(dst[:].rr("p c t -> p (c t)"), p_[:, :], eng="act")
                ck(11)
                k.act(beta[:], sct[:, 4 * d:4 * d + 4], AF.Sigmoid)
                k.tt(gg[:], sct[:, 8 + 4 * d:12 + 4 * d], dtb, ALU.add)
                k.act(gg[:], gg[:], AF.Exp)
                k.act(gg[:], gg[:], AF.Ln, bias=1.0)
                k.tt(gg[:], gg[:], nega[:], ALU.mult)
                pg = k.ps()
                k.mm(pg[:, 0:4], tri[d][:], gg[:])
                k.mm(pg[:, 4:8], ones[:], gg[:])
                k.cp(gcc[:], pg[:, 0:4])
                k.act(egc[:], pg[:, 0:4], AF.Exp)
                k.tt(bege[:], beta[:], egc[:], ALU.mult)
                k.tt(erem[:], pg[:, 4:8], gcc[:], ALU.subtract)
                k.act(erem[:], erem[:], AF.Exp)
                k.act(etot[:], pg[:, 4:8], AF.Exp)
                k.ts(nbeta[:], beta[:], -1.0, None, ALU.mult)
                ck(12)
                ogd_t = ogd[it % 2]
                for h in range(4):
                    kT, qT = qkv[:, 4 + h, :], qkv[:, h, :]
                    k.ts(trg[:], tri[d][:], gg[:, h:h + 1], None, ALU.mult)
                    pG = k.ps()
                    k.mm(pG[:, 0:128], ones[:], trg[:])
                    k.ts(Dm[:], pG[:, 0:128], gcc[:, h:h + 1], 0.0, ALU.subtract, ALU.max)
                    k.act(Dm[:], Dm[:], AF.Exp, scale=-1.0)
                    k.tt(Dm[:], Dm[:], nmsk[d][:], ALU.mult)
                    k.ts(DTm[:], pG[:, 0:128], gcc[:, h:h + 1], 0.0, ALU.subtract, ALU.min)
                    k.act(DTm[:], DTm[:], AF.Exp)
                    k.tt(DTm[:], DTm[:], msk2[d][:, 128:256], ALU.mult)
                    k.act(EG[:], pG[:, 0:128], AF.Exp)
                    pgr = k.ps()
                    k.mm(pgr[:, 0:128], kT, kT)
                    N_, M_ = Nm[h % 2], Mm[h % 2]
                    k.stt(N_[:], pgr[:, 0:128], nbeta[:, h:h + 1], Dm[:], ALU.mult, ALU.mult)
                    pt = k.ps()
                    k.tr(pt[:, 0:128], N_[:], ident[:])
                    k.cp(M_[:], pt[:, 0:128], eng="act")
                    pa = k.ps()
                    k.mm(pa[:, 0:128], kT, qT)
                    k.tt(ATt[:], pa[:, 0:128], DTm[:], ALU.mult)
                    k.tt(QTt[:], qT, EG[:], ALU.mult)
                    k.ts(kbe[:], ktok[:, h, :], bege[:, h:h + 1], None, ALU.mult)
                    k.ts(vb[:], vtok[:, h, :], beta[:, h:h + 1], None, ALU.mult)
                    k.ts(KBt[:], ktok[:, h, :], erem[:, h:h + 1], None, ALU.mult)
                    Tt = inv_chain(e, N_, M_, pool)
                    pw = k.ps()
                    k.mm(pw[:, 0:128], kbe[:], Tt[:])
                    wt = WTt[h % 2]
                    k.ts(wt[:], pw[:, 0:128], -1.0, None, ALU.mult)
                    seq_step(HG[d][h], 128, 128, Tt, vb[:], wt[:], QTt[:], ATt[:], KBt[:],
                             etot[:, h:h + 1], ogd_t, 128 * h, Ut[h % 2])
                    ck(13)
                k.dma(s.OG[d][t0:t0 + 128, :], ogd_t[:], "pool")
                it += 1
                ck(14)

    def fnet(l, sqs):
        with ExitStack() as e:
            c64 = k.sb(e, (128, 128), F32, "c64"); s64 = k.sb(e, (128, 128), F32, "s64")
            k.dma(c64[:], C["c64bd"]); k.dma(s64[:], C["s64bd"])
            PQ = k.sb(e, (128, 2, 256), F32, "PQ")
            for pr in range(2):
                wf = k.sb(e, (128, 128), F32, "wf")
                k.dma(wf[:], I["fn_wbd"][l][pr])
                p_ = k.ps()
                k.mm(p_[:, 0:128], c64[:], wf[:])
                k.mm(p_[:, 128:256], s64[:], wf[:])
                k.cp(PQ[:, pr, :], p_[:, 0:256])
            gts = [k.sb(e, (128, 2, 128), F32, "gt") for _ in range(2)]
            abt = [k.sb(e, (128, 512), F32, "abt") for _ in range(2)]
            for s in sqs:
                for i in range(s.nt):
                    gt = gts[i % 2]
                    k.dma(gt[:], s.UF.rearrange("(c p) t -> p c t", p=128)[:, :, i * 128:(i + 1) * 128], "sp")
                    p_ = k.ps()
                    for pr in range(2):
                        k.mm(p_[:, pr * 128:(pr + 1) * 128], gt[:, pr, :], PQ[:, pr, 0:128])
                        k.mm(p_[:, 256 + pr * 128:256 + (pr + 1) * 128], gt[:, pr, :], PQ[:, pr, 128:256])
                    ab = abt[i % 2]
                    k.cp(ab[:], p_[:, :], eng="act")
                    k.dma(s.AB[i * 128:(i + 1) * 128, :], ab[:], "pool")
            k.barrier()
            for s in sqs:
                if s.isctx:
                    nk = s.nt
                    cc = k.sb(e, (128, nk, s.n), F32, "cc"); sc_ = k.sb(e, (128, nk, s.n), F32, "scc")
                    k.dma(cc[:], C["cc"].rearrange("k p n -> p k n")); k.dma(sc_[:], C["sc_"].rearrange("k p n -> p k n"))
                    k.ts(sc_[:], sc_[:], -1.0, None, ALU.mult)
                    abc = k.sb(e, (128, nk, 512), F32, "abc")
                    k.dma(abc[:], s.AB.rearrange("(k p) n -> p k n", p=128))
                    for mt in range(nk):
                        p_ = k.ps()
                        for kt in range(nk):
                            k.mm(p_[:, 0:256], cc[:, kt, mt * 128:(mt + 1) * 128], abc[:, kt, 0:256],
                                 start=(kt == 0), stop=False)
                            k.mm(p_[:, 0:256], sc_[:, kt, mt * 128:(mt + 1) * 128], abc[:, kt, 256:512],
                                 start=False, stop=(kt == nk - 1))
                        yf = k.sb(e, (128, 256), F32, "yfc")
                        k.cp(yf[:], p_[:, 0:256])
                        k.dma(s.YF[mt * 128:(mt + 1) * 128, :], yf[:], "pool")
                    continue
                c1 = k.sb(e, (128, 128), F32, "c1"); s1 = k.sb(e, (128, 128), F32, "s1"); ns1 = k.sb(e, (128, 128), F32, "ns1")
                nc1 = k.sb(e, (128, 128), F32, "nc1")
                k.dma(c1[:], C["c1"]); k.dma(s1[:], C["s1"])
                k.ts(ns1[:], s1[:], -1.0, None, ALU.mult)
                k.ts(nc1[:], c1[:], -1.0, None, ALU.mult)
                twc = k.sb(e, (128, T2), F32, "twc"); tws = k.sb(e, (128, T2), F32, "tws")
                k.dma(twc[:], C["twc"]); k.dma(tws[:], C["tws"])
                c3 = k.sb(e, (T2, T2), F32, "c3"); s3 = k.sb(e, (T2, T2), F32, "s3")
                k.dma(c3[:], C["c3"]); k.dma(s3[:], C["s3"])
                ABv = s.AB.rearrange("(a b) n -> a b n", b=T2)
                abs_ = [k.sb(e, (128, 2, 512), F32, "abs") for _ in range(2)]
                yre = k.sb(e, (128, 2, 256), F32, "yre"); yim = k.sb(e, (128, 2, 256), F32, "yim")
                zt = [k.sb(e, (128, 2, 2, 256), F32, "zt") for _ in range(2)]
                tmpz = k.sb(e, (128, 2, 256), F32, "tmpz")
                for b2 in range(T2 // 2):
                    ab = abs_[b2 % 2]
                    k.dma(ab[:], ABv[:, 2 * b2:2 * b2 + 2, :], "sp")
                    pre = k.ps(); pim = k.ps()
                    for j in range(2):
                        k.mm(pre[:, j * 256:(j + 1) * 256], c1[:], ab[:, j, 0:256], start=True, stop=False)
                        k.mm(pre[:, j * 256:(j + 1) * 256], ns1[:], ab[:, j, 256:512], start=False, stop=True)
                        k.mm(pim[:, j * 256:(j + 1) * 256], ns1[:], ab[:, j, 0:256], start=True, stop=False)
                        k.mm(pim[:, j * 256:(j + 1) * 256], nc1[:], ab[:, j, 256:512], start=False, stop=True)
                    k.cp(yre[:].rr("p a d -> p (a d)"), pre[:, :], eng="act")
                    k.cp(yim[:].rr("p a d -> p (a d)"), pim[:, :], eng="act")
                    z = zt[b2 % 2]
                    cb = twc[:, 2 * b2:2 * b2 + 2].un(2).bc((128, 2, 256))
                    sb_ = tws[:, 2 * b2:2 * b2 + 2].un(2).bc((128, 2, 256))
                    k.tt(z[:, :, 0, :], yre[:], cb, ALU.mult)
                    k.tt(tmpz[:], yim[:], sb_, ALU.mult)
                    k.tt(z[:, :, 0, :], z[:, :, 0, :], tmpz[:], ALU.add)
                    k.tt(z[:, :, 1, :], yim[:], cb, ALU.mult)
                    k.tt(tmpz[:], yre[:], sb_, ALU.mult)
                    k.tt(z[:, :, 1, :], z[:, :, 1, :], tmpz[:], ALU.subtract)
                    k.dma(ZS[:, 2 * b2:2 * b2 + 2, :, :], z[:], "pool")
                k.barrier()
                ZSv = ZS.rearrange("a b r d -> b a r d")
                YFv = s.YF.rearrange("(b a) d -> b a d", a=128)
                zl = [k.sb(e, (T2, 2, 2, 256), F32, "zl") for _ in range(2)]
                yo = [k.sb(e, (T2, 2, 256), F32, "yo") for _ in range(2)]
                for a2 in range(64):
                    z = zl[a2 % 2]
                    k.dma(z[:], ZSv[:, 2 * a2:2 * a2 + 2, :, :], "sp")
                    p_ = k.ps()
                    for j in range(2):
                        k.mm(p_[0:T2, j * 256:(j + 1) * 256], c3[:], z[:, j, 0, :], start=True, stop=False)
                        k.mm(p_[0:T2, j * 256:(j + 1) * 256], s3[:], z[:, j, 1, :], start=False, stop=True)
                    y = yo[a2 % 2]
                    k.cp(y[:].rr("p a d -> p (a d)"), p_[0:T2, :], eng="act")
                    k.dma(YFv[:, 2 * a2:2 * a2 + 2, :], y[:], "pool")

    def pass4a(l, sqs):
        with ExitStack() as e:
            wo = k.sb(e, (128, 8, D), BF16, "wo")
            k.dma(wo[:], I["w_out"][l].rearrange("(k p) n -> p k n", p=128), "pool")

            def bcast(name, src, n):
                t = k.sb(e, (128, n), F32, name)
                k.dma(t[:], src.partition_broadcast(128), "sp")
                return t
            lng = bcast("lng", I["rk_vec"][l][3], 256)
            lnb = bcast("lnb", I["rk_vec"][l][4], 256)
            gng = bcast("gng", I["gd_ng"][l][0], 128)

            def W(shape, name, dt=F32):
                return k.sb(e, shape, dt, name)
            xts = [W((128, D), "xt4") for _ in range(2)]
            mixs = [W((128, D), "mix") for _ in range(2)]
            mixb = W((128, D), "mixb", BF16)
            o1s = [W((128, 512), "o1") for _ in range(2)]; o2s = [W((128, 512), "o2") for _ in range(2)]
            zts = [W((128, 512), "zt") for _ in range(2)]
            b1s = [W((128, 256), "b1") for _ in range(2)]; b2s = [W((128, 256), "b2") for _ in range(2)]
            gts = [W((128, 256), "gt") for _ in range(2)]
            oas = [W((128, 256), "oa") for _ in range(2)]; obs = [W((128, 256), "ob") for _ in range(2)]
            s4 = W((128, 4), "p4s4"); s4b = W((128, 4), "p4s4b"); s4c = W((128, 4), "p4s4c"); mean = W((128, 4), "mean")
            mixT = W((128, 8, 128), "mixT", BF16)
            yt = W((128, D), "yt")
            it = 0
            for s in sqs:
                wh = 1 if s.isctx else 0
                xsrc = s.xin if l == 0 else s.X
                for i in range(s.nt):
                    t0 = i * 128
                    b_ = it % 2
                    it += 1
                    xt, mix, o1, o2, zt_, b1, b2_, gt_, oa, ob = (xts[b_], mixs[b_], o1s[b_], o2s[b_], zts[b_], b1s[b_],
                                                                 b2s[b_], gts[b_], oas[b_], obs[b_])
                    k.dma(xt[:], xsrc[t0:t0 + 128, :], "sp")
                    k.dma(oa[:], s.OR[0][t0:t0 + 128, :], "sp")
                    k.dma(ob[:], s.OR[1][t0:t0 + 128, :], "sp")
                    k.dma(b1[:], s.BON[0][t0:t0 + 128, :], "sp")
                    k.dma(b2_[:], s.BON[1][t0:t0 + 128, :], "sp")
                    k.dma(gt_[:], s.GATE[t0:t0 + 128, :], "sp")
                    k.dma(o1[:], s.OG[0][t0:t0 + 128, :], "sp")
                    k.dma(o2[:], s.OG[1][t0:t0 + 128, :], "sp")
                    k.dma(zt_[:], s.UT[t0:t0 + 128, 896:1408], "sp")
                    k.dma(mix[:, 768:1024], s.YF[t0:t0 + 128, :], "sp")
                    o = oa[:]
                    k.tt(o, o, ob[:], ALU.add)
                    ov = o.rr("p (h n) -> p h n", h=4)
                    k.op("dve", "tensor_reduce", s4[:], ov, AX.X, ALU.add)
                    k.ts(mean[:], s4[:], 1.0 / 64, None, ALU.mult)
                    k.tt(ov, ov, mean[:].un(2).bc((128, 4, 64)), ALU.subtract)
                    k.tt(ob[:], o, o, ALU.mult)
                    k.op("dve", "tensor_reduce", s4[:], ob[:].rr("p (h n) -> p h n", h=4), AX.X, ALU.add)
                    k.rsqrt(s4b[:], s4[:], 1.0 / 64, 64e-5, s4c[:])
                    k.tt(ov, ov, s4b[:].un(2).bc((128, 4, 64)), ALU.mult)
                    k.tt(o, o, lng[:], ALU.mult)
                    k.tt(o, o, lnb[:], ALU.add)
                    k.tt(o, o, b1[:], ALU.add)
                    k.tt(o, o, b2_[:], ALU.add)
                    k.tt(mix[:, 0:256], o, gt_[:], ALU.mult)
                    k.tt(o1[:], o1[:], o2[:], ALU.add)
                    k.tt(o2[:], o1[:], o1[:], ALU.mult)
                    k.op("dve", "tensor_reduce", s4[:], o2[:].rr("p (h n) -> p h n", h=4), AX.X, ALU.add)
                    k.rsqrt(s4b[:], s4[:], 1.0 / 128, 1e-6, s4c[:])
                    og = o1[:].rr("p (h n) -> p h n", h=4)
                    k.tt(og, og, s4b[:].un(2).bc((128, 4, 128)), ALU.mult)
                    k.tt(og, og, gng[:].un(1).bc((128, 4, 128)), ALU.mult)
                    k.act(zt_[:], zt_[:], AF.Silu)
                    k.tt(mix[:, 256:768], o1[:], zt_[:], ALU.mult)
                    k.cp(mixb[:], mix[:], eng="pool")
                    p_ = k.ps()
                    pb = p_[:, :].bitcast(BF16)
                    for kc in range(8):
                        k.tr(pb[:, kc * 128:(kc + 1) * 128], mixb[:, kc * 128:(kc + 1) * 128], identb[:])
                    k.cp(mixT[:].rr("p k t -> p (k t)"), pb, eng="act")
                    for hb in range(2):
                        p_ = k.ps()
                        for kc in range(8):
                            k.mm(p_[:, :], mixT[:, kc, :], wo[:, kc, hb * 512:(hb + 1) * 512],
                                 start=(kc == 0), stop=(kc == 7))
                        k.tt(yt[:, hb * 512:(hb + 1) * 512], p_[:, :], M2[wh][:, hb * 512:(hb + 1) * 512], ALU.mult)
                    k.tt(xt[:], xt[:], yt[:], ALU.add)
                    k.dma(s.X[t0:t0 + 128, :], xt[:], "pool")

    def pass4b(l, sqs, lastl):
        with ExitStack() as e:
            w1 = k.sb(e, (128, 8, 4096), BF16, "w1")
            w2 = k.sb(e, (128, 32, D), BF16, "w2")
            s1v = I["mlp_w1"][l].rearrange("(k p) n -> p k n", p=128)
            for kc in range(8):
                for h in range(2):
                    k.dma(w1[:, kc, h * 2048:(h + 1) * 2048], s1v[:, kc, h * 2048:(h + 1) * 2048], "pool")
            s2v = I["mlp_w2"][l].rearrange("(k p) n -> p k n", p=128)
            for kq in range(8):
                k.dma(w2[:, kq * 4:(kq + 1) * 4, :], s2v[:, kq * 4:(kq + 1) * 4, :], "pool")
            fg = None
            if lastl:
                fg = k.sb(e, (128, D), F32, "fg")
                k.dma(fg[:], I["final_g"][0].partition_broadcast(128), "sp")

            def W(shape, name, dt=F32):
                return k.sb(e, shape, dt, name)
            GSM = 256
            xts = [W((128, D), "xt4") for _ in range(2)]
            hT = W((128, 8, GSM), "hT4", BF16)
            aT = W((128, 32, GSM), "aT", BF16)
            yt = W((128, D), "yt")
            tr_ = W((128, GSM), "tr_")
            tmp_pool = (yt, W((128, 1), "ssq4"), W((128, 1), "sq4"), W((128, 1), "rstd4"), W((128, D), "xnb4", BF16))
            for s in sqs:
                wh = 1 if s.isctx else 0
                GS = min(GSM, s.n)
                ntg = GS // 128
                for g0 in range(0, s.n, GS):
                    for j in range(ntg):
                        t0 = g0 + j * 128
                        xt = xts[j]
                        k.dma(xt[:], s.X[t0:t0 + 128, :], "sp")
                        norm_to_hT(e, xt, hT, j * 128, G2[wh], SH2[wh], tmp_pool)
                    for fc in range(32):
                        p_ = k.ps()
                        for kc in range(8):
                            k.mm(p_[:, 0:GS], w1[:, kc, fc * 128:(fc + 1) * 128], hT[:, kc, 0:GS],
                                 start=(kc == 0), stop=(kc == 7))
                        k.act(tr_[:, 0:GS], p_[:, 0:GS], AF.Relu)
                        k.tt(aT[:, fc, 0:GS], tr_[:, 0:GS], tr_[:, 0:GS], ALU.mult)
                    for j in range(ntg):
                        t0 = g0 + j * 128
                        xt = xts[j]
                        for hb in range(2):
                            p_ = k.ps()
                            for fc in range(32):
                                k.mm(p_[:, :], aT[:, fc, j * 128:(j + 1) * 128], w2[:, fc, hb * 512:(hb + 1) * 512],
                                     start=(fc == 0), stop=(fc == 31))
                            k.tt(yt[:, hb * 512:(hb + 1) * 512], p_[:, :], M5[wh][:, hb * 512:(hb + 1) * 512], ALU.mult)
                        k.tt(xt[:], xt[:], yt[:], ALU.add)
                        if lastl:
                            junk, ssq, sq, rstd, _ = tmp_pool
                            k.act(junk[:], xt[:], AF.Square, accum_out=ssq[:])
                            k.rsqrt(rstd[:], ssq[:], 1.0 / D, 1e-6, sq[:])
                            k.stt(yt[:], xt[:], rstd[:, 0:1], fg[:], ALU.mult, ALU.mult)
                            k.dma(OUT[t0:t0 + 128, :], yt[:], "pool")
                        else:
                            k.dma(s.X[t0:t0 + 128, :], xt[:], "pool")

    k.barrier()
    for l in range(L):
        lastl = l == L - 1
        pass0_mod(l)
        k.barrier()
        pass1(l, [SC, SL])
        k.barrier()
        if upto < 2:
            break
        try:
            for d in range(2):
                scan_pass(l, d)
                k.barrier()
        except Cut:
            k.barrier()
            return nc, cst
        if upto < 3:
            break
        sq4 = [SL] if lastl else [SC, SL]
        fnet(l, sq4)
        k.barrier()
        if upto < 4:
            break
        pass4a(l, sq4)
        k.barrier()
        pass4b(l, sq4, lastl)
        k.barrier()
        if upto < 5:
            break
    es.close()
    return nc, cst


def host_layout(inputs, b, L):
    f = lambda a: np.ascontiguousarray(np.asarray(a, dtype=np.float32))
    m = {}
    m["x"] = f(inputs["x"][b])
    m["ctx"] = f(inputs["ctx"][b])
    m["cvec"] = f(np.concatenate([np.asarray(inputs["c"][b]).reshape(8, 128), np.asarray(inputs["c_ctx"]).reshape(8, 128)], 0))
    m["final_g"] = f(np.asarray(inputs["final_g"]).reshape(1, D))
    m["norm_g"] = f(np.concatenate([np.asarray(inputs["norm1_g"]).reshape(L, 8, 128),
                                    np.asarray(inputs["norm2_g"]).reshape(L, 8, 128)], 1))
    m["w_mod"] = f(inputs["w_mod"])
    m["b_mod"] = f(np.asarray(inputs["b_mod"]).reshape(L, 1, 6144))
    perm_r = np.concatenate([np.arange(s_, 896, 4) for s_ in range(4)])
    cols = np.concatenate([perm_r, np.arange(2432, 2944), np.arange(2944, 2960), np.arange(896, 2432),
                           np.arange(2960, 3216)])
    m["w_in"] = f(np.asarray(inputs["w_in"])[:, :, cols])
    m["w_out"] = f(inputs["w_out"])
    m["rk_mu"] = f(np.asarray(inputs["rk_mu"])[:, perm_r].reshape(L, 1, 896))
    m["rk_wup"] = f(np.concatenate([np.asarray(inputs["rk_w_up"]), np.asarray(inputs["rk_w0"])[:, :, None, :]], 2))
    m["rk_aup"] = f(np.concatenate([np.asarray(inputs["rk_a_up"]), np.asarray(inputs["rk_a0"])[:, :, None, :]], 2))
    m["rk_gup"] = f(inputs["rk_g_up"])
    m["rk_vec"] = f(np.stack([np.asarray(inputs["rk_k_k"]), np.asarray(inputs["rk_k_a"]),
                              np.asarray(inputs["rk_r_k"]).reshape(L, 256), np.asarray(inputs["rk_lnx_g"]),
                              np.asarray(inputs["rk_lnx_b"])], 1))
    m["gd_convw"] = f(np.transpose(np.asarray(inputs["gd_conv_w"]), (0, 2, 1)))
    m["gd_sc"] = f(np.concatenate([np.asarray(inputs["gd_a_log"]).reshape(L, 8),
                                   np.asarray(inputs["gd_dt_bias"]).reshape(L, 8)], 1).reshape(L, 1, 16))
    m["gd_ng"] = f(np.asarray(inputs["gd_norm_g"]).reshape(L, 1, 128))
    fw = np.asarray(inputs["fn_w"])
    wbd = np.zeros((L, 2, 128, 128), np.float32)
    for pr in range(2):
        wbd[:, pr, 0:64, 0:64] = fw[:, 2 * pr]
        wbd[:, pr, 64:128, 64:128] = fw[:, 2 * pr + 1]
    m["fn_wbd"] = wbd
    m["mlp_w1"] = f(inputs["mlp_w1"])
    m["mlp_w2"] = f(inputs["mlp_w2"])
    return m


_CACHE = {}


def kernel(**inputs):
    B, T, _ = inputs["x"].shape
    TC = inputs["ctx"].shape[1]
    L = inputs["w_mod"].shape[0]
    key = (T, TC, L)
    if key not in _CACHE:
        _CACHE[key] = build(T, TC, L)
    nc, cst = _CACHE[key]
    in_maps = []
    for core in range(8):
        b = core % B
        m = host_layout(inputs, b, L)
        for k_, v in cst.items():
            m["c_" + k_] = v
        in_maps.append(m)
    res = run_bass_kernel_spmd(nc, in_maps, core_ids=list(range(8)))
    out = np.stack([np.asarray(res.results[b]["out"], dtype=np.float32) for b in range(B)], 0)
    return out
```
